# Optimizing a Trainium2 kernel written in Bass

```python
import math
import jax, jax.numpy as jnp
from jax import lax
import numpy as np

D_MODEL = 1024
BATCH = 8
SEQ = 2048
DEPTH = 1
DEC_BATCH = 2
DEC_SEQ = 8192
PAST_LEN = 128

N_META = 16
GRID_W = 64
MIX_WIDTH = D_MODEL
POOL_GROUPS = 4
POOL_WIDTH = MIX_WIDTH // 2
POOL_CH = POOL_WIDTH // POOL_GROUPS
POOL_WINDOWS = (2, 4, 8, 16)
NAT_WIDTH = MIX_WIDTH - POOL_WIDTH
NAT_HEADS = 8
NAT_HEAD_DIM = NAT_WIDTH // NAT_HEADS
NAT_KH = 8
NAT_KW = 16
IN_WIDTH = POOL_WIDTH + 3 * NAT_WIDTH
PEER_HEADS = 8
PEER_NKEYS = 128
PEER_EXPERTS = PEER_NKEYS * PEER_NKEYS
PEER_TOPK = 16
PEER_DK = 256
PEER_DK_HALF = PEER_DK // 2
PEER_CHUNK = 512
LN_EPS = 1e-5
NEG_INF = -1e30
DEEPNORM_ALPHA = float((2 * DEPTH) ** 0.25)
DEEPNORM_BETA = float((8 * DEPTH) ** -0.25)

kernel_name = "hybrid_pool_natten_peer_encoder"


def layer_norm(x, g, b):
    xf = x.astype(jnp.float32)
    mu = jnp.mean(xf, axis=-1, keepdims=True)
    var = jnp.mean(jnp.square(xf - mu), axis=-1, keepdims=True)
    y = (xf - mu) * lax.rsqrt(var + LN_EPS)
    return (y * g.astype(jnp.float32) + b.astype(jnp.float32)).astype(x.dtype)


def pool_mixer(z, pool_w, pool_scale):
    B, L, _ = z.shape
    zg = z.reshape(B, L, POOL_GROUPS, POOL_CH)
    zf = zg.astype(jnp.float32)
    cs = jnp.concatenate([jnp.zeros((B, 1, POOL_GROUPS, POOL_CH), jnp.float32),
                          jnp.cumsum(zf, axis=1)], axis=1)
    t = jnp.arange(L)[:, None]
    w = jnp.array(POOL_WINDOWS, dtype=jnp.int32)[None, :]
    lo = jnp.clip(t - w // 2, 0, L - 1)
    hi = jnp.clip(t - w // 2 + w - 1, 0, L - 1)
    g_idx = jnp.arange(POOL_GROUPS)[None, :]
    tot = cs[:, hi + 1, g_idx] - cs[:, lo, g_idx]
    count = (hi - lo + 1).astype(jnp.float32)[None, :, :, None]
    pooled = (tot / count - zf).astype(z.dtype)
    y = jnp.einsum('blgc,gcd->blgd', pooled, pool_w).reshape(B, L, POOL_WIDTH)
    return y * pool_scale


def neighbourhood_attention(q, k, v, rpb):
    B, L, H, dh = q.shape
    T = L - N_META
    rows = T // GRID_W
    kh = min(NAT_KH, rows)
    scale = dh ** -0.5
    qm, km, vm = q[:, :N_META], k[:, :N_META], v[:, :N_META]
    qr = q[:, N_META:].reshape(B, rows, GRID_W, H, dh)
    kr = k[:, N_META:].reshape(B, rows, GRID_W, H, dh)
    vr = v[:, N_META:].reshape(B, rows, GRID_W, H, dh)
    r = jnp.arange(rows)
    r_start = jnp.clip(r - kh // 2, 0, rows - kh)
    row_idx = r_start[:, None] + jnp.arange(kh)[None, :]
    k_blk = kr[:, row_idx]
    v_blk = vr[:, row_idx]
    c = jnp.arange(GRID_W)
    c_start = jnp.clip(c - NAT_KW // 2, 0, GRID_W - NAT_KW)
    col_valid = (c[None, :] >= c_start[:, None]) & (c[None, :] < c_start[:, None] + NAT_KW)
    row_off = row_idx - r[:, None] + (NAT_KH - 1)
    col_off = jnp.clip(c[None, :] - c[:, None] + (NAT_KW - 1), 0, 2 * NAT_KW - 2)
    bias = rpb[:, row_off[:, None, :, None], col_off[None, :, None, :]]
    s_win = jnp.einsum('brqhd,brkwhd->bhrqkw', qr, k_blk).astype(jnp.float32) * scale
    s_win = s_win + bias.astype(jnp.float32)
    s_win = jnp.where(col_valid[:, None, :], s_win, NEG_INF)
    s_meta = jnp.einsum('brqhd,bmhd->bhrqm', qr, km).astype(jnp.float32) * scale
    s = jnp.concatenate([s_win.reshape(B, H, rows, GRID_W, kh * GRID_W), s_meta], axis=-1)
    p = jax.nn.softmax(s, axis=-1).astype(v.dtype)
    p_win = p[..., :kh * GRID_W].reshape(B, H, rows, GRID_W, kh, GRID_W)
    p_meta = p[..., kh * GRID_W:]
    o_r = (jnp.einsum('bhrqkw,brkwhd->brqhd', p_win, v_blk)
           + jnp.einsum('bhrqm,bmhd->brqhd', p_meta, vm)).reshape(B, T, H, dh)
    s_mm = jnp.einsum('bqhd,bkhd->bhqk', qm, km).astype(jnp.float32) * scale
    p_mm = jax.nn.softmax(s_mm, axis=-1).astype(v.dtype)
    o_m = jnp.einsum('bhqk,bkhd->bqhd', p_mm, vm)
    return jnp.concatenate([o_m, o_r], axis=1).reshape(B, L, H * dh)


def peer_ffn(h, wq, key1, key2, u_tab, v_tab):
    B, L, D = h.shape
    n = B * L
    pad = (-n) % PEER_CHUNK
    xt = jnp.pad(h.reshape(n, D), ((0, pad), (0, 0))).reshape(-1, PEER_CHUNK, D)

    def block(xc):
        q = (xc @ wq).reshape(PEER_CHUNK, PEER_HEADS, 2, PEER_DK_HALF)
        s1 = jnp.einsum('chk,hnk->chn', q[:, :, 0], key1).astype(jnp.float32)
        s2 = jnp.einsum('chk,hnk->chn', q[:, :, 1], key2).astype(jnp.float32)
        t1, i1 = lax.top_k(s1, PEER_TOPK)
        t2, i2 = lax.top_k(s2, PEER_TOPK)
        cand_s = (t1[..., :, None] + t2[..., None, :]).reshape(PEER_CHUNK, PEER_HEADS, PEER_TOPK * PEER_TOPK)
        cand_i = (i1[..., :, None] * PEER_NKEYS + i2[..., None, :]).reshape(PEER_CHUNK, PEER_HEADS, PEER_TOPK * PEER_TOPK)
        top_s, pos = lax.top_k(cand_s, PEER_TOPK)
        e_idx = jnp.take_along_axis(cand_i, pos, axis=-1).reshape(PEER_CHUNK, PEER_HEADS * PEER_TOPK)
        g = jax.nn.softmax(top_s, axis=-1).reshape(PEER_CHUNK, PEER_HEADS * PEER_TOPK)
        u_g = u_tab[e_idx]
        a = jnp.einsum('cd,ced->ce', xc, u_g).astype(jnp.float32)
        a = (jax.nn.gelu(a, approximate=False) * g).astype(xc.dtype)
        return jnp.einsum('ce,ced->cd', a, v_tab[e_idx])

    y = lax.map(block, xt).reshape(-1, D)[:n]
    return y.reshape(B, L, D)


def encode(x, meta_tokens, emb_ln_g, emb_ln_b, w_in, pool_w, pool_scale, nat_rpb, w_out,
           ln1_g, ln1_b, peer_wq, peer_key1, peer_key2, peer_u, peer_v, ln2_g, ln2_b):
    B = x.shape[0]
    meta = jnp.broadcast_to(meta_tokens[None], (B, N_META, D_MODEL)).astype(x.dtype)
    h = layer_norm(jnp.concatenate([meta, x], axis=1), emb_ln_g, emb_ln_b)
    L = h.shape[1]
    for i in range(DEPTH):
        z = h @ w_in[i]
        zp = z[..., :POOL_WIDTH]
        q = z[..., POOL_WIDTH:POOL_WIDTH + NAT_WIDTH].reshape(B, L, NAT_HEADS, NAT_HEAD_DIM)
        k = z[..., POOL_WIDTH + NAT_WIDTH:POOL_WIDTH + 2 * NAT_WIDTH].reshape(B, L, NAT_HEADS, NAT_HEAD_DIM)
        v = z[..., POOL_WIDTH + 2 * NAT_WIDTH:].reshape(B, L, NAT_HEADS, NAT_HEAD_DIM)
        a_out = pool_mixer(zp, pool_w[i], pool_scale[i])
        b_out = neighbourhood_attention(q, k, v, nat_rpb[i])
        mix = jnp.concatenate([a_out, b_out], axis=-1) @ w_out[i]
        h = layer_norm(DEEPNORM_ALPHA * h + mix, ln1_g[i], ln1_b[i])
        f = peer_ffn(h, peer_wq[i], peer_key1[i], peer_key2[i], peer_u[i], peer_v[i])
        h = layer_norm(DEEPNORM_ALPHA * h + f, ln2_g[i], ln2_b[i])
    return h[:, N_META:]


def setup_inputs(seed: int = 0) -> dict:
    key = jax.random.key(seed)
    ks = jax.random.split(key, 20)
    f32 = jnp.float32
    D = D_MODEL
    x_prompt = jax.random.normal(ks[0], (BATCH, SEQ, D), f32)
    x_sample = jax.random.normal(ks[1], (DEC_BATCH, DEC_SEQ, D), f32)
    meta_tokens = jax.random.normal(ks[2], (N_META, D), f32)
    emb_ln_g = 1.0 + 0.02 * jax.random.normal(ks[3], (D,), f32)
    emb_ln_b = 0.02 * jax.random.normal(ks[4], (D,), f32)
    col_scale = jnp.concatenate([jnp.ones((POOL_WIDTH + 2 * NAT_WIDTH,), f32),
                                 jnp.full((NAT_WIDTH,), DEEPNORM_BETA, f32)])
    w_in = jax.random.normal(ks[5], (DEPTH, D, IN_WIDTH), f32) * (D ** -0.5) * col_scale
    pool_w = jax.random.normal(ks[6], (DEPTH, POOL_GROUPS, POOL_CH, POOL_CH), f32) * (POOL_CH ** -0.5)
    pool_scale = 1.0 + 0.1 * jax.random.normal(ks[7], (DEPTH, POOL_WIDTH), f32)
    nat_rpb = 0.02 * jax.random.normal(ks[8], (DEPTH, NAT_HEADS, 2 * NAT_KH - 1, 2 * NAT_KW - 1), f32)
    w_out = jax.random.normal(ks[9], (DEPTH, MIX_WIDTH, D), f32) * (MIX_WIDTH ** -0.5) * DEEPNORM_BETA
    ln1_g = 1.0 + 0.02 * jax.random.normal(ks[10], (DEPTH, D), f32)
    ln1_b = 0.02 * jax.random.normal(ks[11], (DEPTH, D), f32)
    peer_wq = jax.random.normal(ks[12], (DEPTH, D, PEER_HEADS * PEER_DK), f32) * (D ** -0.5)
    peer_key1 = jax.random.normal(ks[13], (DEPTH, PEER_HEADS, PEER_NKEYS, PEER_DK_HALF), f32) * (PEER_DK_HALF ** -0.5)
    peer_key2 = jax.random.normal(ks[14], (DEPTH, PEER_HEADS, PEER_NKEYS, PEER_DK_HALF), f32) * (PEER_DK_HALF ** -0.5)
    peer_u = jax.random.normal(ks[15], (DEPTH, PEER_EXPERTS, D), f32) * (D ** -0.5)
    peer_v = jax.random.normal(ks[16], (DEPTH, PEER_EXPERTS, D), f32) * DEEPNORM_BETA * (PEER_HEADS ** -0.5)
    ln2_g = 1.0 + 0.02 * jax.random.normal(ks[17], (DEPTH, D), f32)
    ln2_b = 0.02 * jax.random.normal(ks[18], (DEPTH, D), f32)
    return {"x_prompt": x_prompt, "x_sample": x_sample, "meta_tokens": meta_tokens,
            "emb_ln_g": emb_ln_g, "emb_ln_b": emb_ln_b, "w_in": w_in, "pool_w": pool_w,
            "pool_scale": pool_scale, "nat_rpb": nat_rpb, "w_out": w_out,
            "ln1_g": ln1_g, "ln1_b": ln1_b, "peer_wq": peer_wq, "peer_key1": peer_key1,
            "peer_key2": peer_key2, "peer_u": peer_u, "peer_v": peer_v,
            "ln2_g": ln2_g, "ln2_b": ln2_b}


def reference(x_prompt, x_sample, meta_tokens, emb_ln_g, emb_ln_b, w_in, pool_w, pool_scale,
              nat_rpb, w_out, ln1_g, ln1_b, peer_wq, peer_key1, peer_key2, peer_u, peer_v,
              ln2_g, ln2_b):
    y_prompt = encode(x_prompt, meta_tokens, emb_ln_g, emb_ln_b, w_in, pool_w, pool_scale, nat_rpb,
                      w_out, ln1_g, ln1_b, peer_wq, peer_key1, peer_key2, peer_u, peer_v, ln2_g, ln2_b)
    y_sample = encode(x_sample, meta_tokens, emb_ln_g, emb_ln_b, w_in, pool_w, pool_scale, nat_rpb,
                      w_out, ln1_g, ln1_b, peer_wq, peer_key1, peer_key2, peer_u, peer_v, ln2_g, ln2_b)
    return (y_prompt, y_sample)
```

```python
import numpy as np
import ml_dtypes
from contextlib import ExitStack
import concourse.bass as bass
import concourse.mybir as mybir
from concourse.bass_utils import run_bass_kernel_spmd

F32 = mybir.dt.float32
BF16 = mybir.dt.bfloat16
U32 = mybir.dt.uint32
U8 = mybir.dt.uint8
ALU = mybir.AluOpType
AF = mybir.ActivationFunctionType
AX = mybir.AxisListType

ALPHA = float(2.0 ** 0.25)
EPS = 1e-5
NEG = -1e30
NSEG = 8
NT = 9
NSLOT = 9
ENG = ['pe', 'act', 'dve', 'pool', 'sp']
DEBUG_H2 = False
DEBUG_WHAT = 'h2'
STOP_AFTER = None


class Tok:
    __slots__ = ('w', 'r')

    def __init__(self):
        self.w = None
        self.r = {}


class Sch:
    def __init__(self, ndma=16):
        self.q = {e: [] for e in ENG}
        self.n = {e: 0 for e in ENG}
        self.dcount = [0] * ndma
        self.dnext = 0
        self.pnext = 0

    def _deps(self, eng, r, w):
        need = {}
        nxt = self.n.get(eng, 0) + 1

        def add(k, i):
            if k == 'pe' and eng == 'pe':
                return
            if k == eng and eng in ('dve', 'act') and nxt - i >= 4:
                return
            if need.get(k, 0) < i:
                need[k] = i
        for t in r:
            if t.w is not None:
                add(*t.w)
        for t in w:
            if t.w is not None:
                add(*t.w)
            for k, i in t.r.items():
                add(k, i)
        return need

    def op(self, eng, fn, r=(), w=(), w_nodep=()):
        need = self._deps(eng, r, w)
        self.n[eng] += 1
        idx = self.n[eng]
        self.q[eng].append(dict(waits=need, fn=fn, kind='c', idx=idx))
        for t in r:
            t.r[eng] = idx
        for t in w:
            t.w = (eng, idx)
            t.r = {}
        for t in w_nodep:
            t.w = (eng, idx)

    def dma(self, eng, fn, r=(), w=()):
        need = self._deps(eng, r, w)
        if eng == 'pool':
            d = 12 + self.pnext
            self.pnext = (self.pnext + 1) % 4
        else:
            d = self.dnext
            self.dnext = (self.dnext + 1) % 12
        key = ('d', d)
        if self.dcount[d] > 0:
            need[key] = max(need.get(key, 0), self.dcount[d])
        self.dcount[d] += 16
        val = self.dcount[d]
        self.q[eng].append(dict(waits=need, fn=fn, kind='d', dsem=d))
        for t in r:
            t.r[key] = val
        for t in w:
            t.w = (key, val)
            t.r = {}

    def barrier(self):
        state = {e: self.n[e] for e in ENG if self.n[e] > 0}
        for d, c in enumerate(self.dcount):
            if c > 0:
                state[('d', d)] = c
        for e in ENG:
            need = {k: v for k, v in state.items() if k != e}
            self.q[e].append(dict(waits=need, fn=None, kind='w'))

    def emit(self, nc, es):
        sems = {e: es.enter_context(nc.semaphore("s_" + e)) for e in ENG}
        dsems = [es.enter_context(nc.semaphore("d_%d" % i)) for i in range(len(self.dcount))]
        fin = {('d', d): c for d, c in enumerate(self.dcount) if c > 0}
        self.q['sp'].append(dict(waits=fin, fn=None, kind='w'))
        waited = {e: set() for e in ENG}
        for e in ENG:
            for it in self.q[e]:
                for k, i in it['waits'].items():
                    if isinstance(k, str):
                        waited[k].add(i)
        ms = {e: {i: m + 1 for m, i in enumerate(sorted(waited[e]))} for e in ENG}
        q = self.q

        def run(e, h):
            seen = {}
            for it in q[e]:
                for k, i in it['waits'].items():
                    if isinstance(k, str):
                        sem, val = sems[k], ms[k][i]
                    else:
                        sem, val = dsems[k[1]], i
                    if seen.get(k, 0) >= val:
                        continue
                    seen[k] = val
                    h.wait_ge(sem, val)
                if it['kind'] == 'w':
                    continue
                ins = it['fn'](h)
                if it['kind'] == 'd':
                    ins.then_inc(dsems[it['dsem']], 16)
                elif it['idx'] in ms[e]:
                    ins.then_inc(sems[e], 1)

        with nc.Block() as block:
            @block.tensor
            def _(h):
                run('pe', h)

            @block.scalar
            def _(h):
                run('act', h)

            @block.vector
            def _(h):
                run('dve', h)

            @block.gpsimd
            def _(h):
                run('pool', h)

            @block.sync
            def _(h):
                run('sp', h)


class Buf:
    def __init__(self, ap, is_handle=False):
        self.ap = ap[:] if is_handle else ap
        self.t = Tok()


def build_program():
    nc = bass.Bass("TRN2", target_bir_lowering=False)
    S = Sch()

    def din(name, shape, dt=F32):
        return nc.dram_tensor(name, list(shape), dt, kind="ExternalInput").ap()

    xin = din("xin", [NSEG, NT, 128, 1024])
    w_in_d = din("w_in", [1024, 2048])
    w_out_d = din("w_out", [1024, 1024])
    wq_d = din("wq", [1024, 2048])
    keysT_d = din("keysT", [128, 16, 128])
    poolw_d = din("pool_w", [4, 128, 128])
    pscale_d = din("pscale", [128, 4])
    lntab_d = din("lntab", [6, 128, 1024])
    bias_d = din("biastab", [NSLOT, 8, 128, 8, 128])
    postv_d = din("postv", [128, NSEG, 8])
    invc_d = din("invc", [128, NSEG, 4, 8])
    ident_d = din("ident", [128, 128])
    iota128_d = din("iota128", [128, 128], BF16)
    iota16_d = din("iota16", [128, 32])
    UT_d = din("UT", [128, 131072])
    V_d = din("Vt", [128, 131072])
    y = nc.dram_tensor("y", [4096, 1024], F32, kind="ExternalOutput").ap()
    if DEBUG_H2:
        dbg = nc.dram_tensor("dbg", [4096, 1024], F32, kind="ExternalOutput").ap()
    Ub = nc.dram_tensor("Ub", [128, 131072], BF16, kind="Internal").ap()
    Vb = nc.dram_tensor("Vb", [128, 131072], BF16, kind="Internal").ap()
    Winb = nc.dram_tensor("Winb", [128, 8, 2048], BF16, kind="Internal").ap()
    Wqb = nc.dram_tensor("Wqb", [128, 8, 2048], BF16, kind="Internal").ap()

    es = ExitStack()

    def sb(name, shape, dt):
        return Buf(es.enter_context(nc.sbuf_tensor("sb_" + name, list(shape), dt)), True)

    ident = sb("ident", [128, 128], F32)
    iota128 = sb("iota128", [128, 128], BF16)
    iota16 = sb("iota16", [128, 32], F32)
    pscale = sb("pscale", [128, 4], F32)
    postv = sb("postv", [128, NSEG, 8], F32)
    invc = sb("invc", [128, NSEG, 4, 8], F32)
    woutb = sb("woutb", [128, 8, 1024], BF16)
    keysb = sb("keysb", [128, 16, 128], BF16)
    poolwb = sb("poolwb", [128, 4, 128], BF16)
    h2s = sb("h2s", [128, 4, 1024], F32)
    h2T = sb("h2T", [128, 8, 512], BF16)
    i1T = sb("i1T", [128, 512], BF16)
    i2T = sb("i2T", [128, 512], BF16)
    gT = sb("gT", [128, 512], BF16)
    small = [sb("small%d" % i, [128, 16], F32) for i in range(4)]
    OVB = 156 * 1024
    OFF_W = 116 * 1024
    OV = es.enter_context(nc.sbuf_tensor("OV", [128, OVB], U8))
    PB = [Buf(es.enter_context(nc.psum_tensor("pb%d" % i, [128, 1024], F32)), True) for i in range(4)]
    pbi = [0]
    pbn = [4]

    def psum():
        b = PB[pbi[0] % pbn[0]]
        pbi[0] += 1
        return b

    class Carver:
        def __init__(self, limit=None):
            self.off = 0
            self.limit = OVB if limit is None else limit

        def at(self, off, shape, dt):
            save = (self.off, self.limit)
            self.off, self.limit = off, OVB
            b = self.get(shape, dt)
            self.off, self.limit = save
            return b

        def get(self, shape, dt):
            sz = {F32: 4, BF16: 2, U32: 4, U8: 1}[dt]
            n = int(np.prod(shape[1:]))
            nb = n * sz
            off = (self.off + 31) // 32 * 32
            assert off + nb <= self.limit, (off, nb, self.limit)
            self.off = off + nb
            v = OV[:, off:off + nb]
            if dt != U8:
                v = v.bitcast(dt)
            if len(shape) == 3:
                v = v.rearrange("p (a b) -> p a b", a=shape[1])
            elif len(shape) == 4:
                v = v.rearrange("p (a b c) -> p a b c", a=shape[1], b=shape[2])
            return Buf(v)

    def dma(out, in_, r=(), w=(), eng='sp'):
        S.dma(eng, lambda h: h.dma_start(out=out, in_=in_), r, w)

    def tcopy(eng, out, in_, r, w):
        if eng == 'act':
            S.op('act', lambda h: h.copy(out=out, in_=in_), r, w)
        else:
            S.op(eng, lambda h: h.tensor_copy(out=out, in_=in_), r, w)

    def tt(eng, out, a, b, op, r, w):
        S.op(eng, lambda h: h.tensor_tensor(out=out, in0=a, in1=b, op=op), r, w)

    def ts(eng, out, a, s1, s2, op0, op1, r, w):
        if s2 is None:
            S.op(eng, lambda h: h.tensor_scalar(out=out, in0=a, scalar1=s1, scalar2=None, op0=op0), r, w)
        else:
            S.op(eng, lambda h: h.tensor_scalar(out=out, in0=a, scalar1=s1, scalar2=s2, op0=op0, op1=op1), r, w)

    def stt(out, a, s, b, op0, op1, r, w):
        S.op('dve', lambda h: h.scalar_tensor_tensor(out=out, in0=a, scalar=s, in1=b, op0=op0, op1=op1), r, w)

    def act(out, in_, func, r, w, scale=None):
        if scale is None:
            S.op('act', lambda h: h.activation(out=out, in_=in_, func=func), r, w)
        else:
            S.op('act', lambda h: h.activation(out=out, in_=in_, func=func, scale=scale), r, w)

    def mm(out, lhsT, rhs, start, stop, r, w):
        S.op('pe', lambda h: h.matmul(out, lhsT, rhs, start=start, stop=stop), r, w)

    def tr(out, in_, r, w):
        S.op('pe', lambda h: h.transpose(out, in_, ident.ap), list(r) + [ident.t], w)

    smi = [0]

    def ln_stats(x_ap, x_toks):
        sm = small[smi[0] % 4]
        smi[0] += 1
        S.op('dve', lambda h: h.bn_stats(out=sm.ap[:, 0:6], in_=x_ap[:, 0:512]), x_toks, [sm.t])
        S.op('dve', lambda h: h.bn_stats(out=sm.ap[:, 6:12], in_=x_ap[:, 512:1024]), x_toks, [sm.t])
        S.op('dve', lambda h: h.bn_aggr(out=sm.ap[:, 12:14], in_=sm.ap[:, 0:12]), [sm.t], [sm.t])
        ts('dve', sm.ap[:, 14:15], sm.ap[:, 13:14], EPS, None, ALU.add, None, [sm.t], [sm.t])
        act(sm.ap[:, 15:16], sm.ap[:, 14:15], AF.Ln, [sm.t], [sm.t])
        act(sm.ap[:, 14:15], sm.ap[:, 15:16], AF.Exp, [sm.t], [sm.t], scale=-0.5)
        return sm

    def ln_apply(sm, x_ap, x_toks, out_ap, out_tok, g, b, scr):
        ts('dve', scr.ap, x_ap, sm.ap[:, 12:13], sm.ap[:, 14:15], ALU.subtract, ALU.mult,
           list(x_toks) + [sm.t], [scr.t])
        tt('dve', scr.ap, scr.ap, g.ap, ALU.mult, [scr.t, g.t], [scr.t])
        tt('dve', out_ap, scr.ap, b.ap, ALU.add, [scr.t, b.t], [out_tok])

    def layer_norm(x_ap, x_toks, out_ap, out_tok, g, b, scr):
        sm = ln_stats(x_ap, x_toks)
        ln_apply(sm, x_ap, x_toks, out_ap, out_tok, g, b, scr)

    def transpose8(src_ap, src_tok, dst_fn, dst_tok):
        for half in range(2):
            pb = psum()
            for kk in range(4):
                k = half * 4 + kk
                tr(pb.ap[:, kk * 128:(kk + 1) * 128], src_ap[:, k * 128:(k + 1) * 128], [src_tok], [pb.t])
            tcopy('act', dst_fn(half * 4, half * 4 + 4),
                  pb.ap[:, 0:512].rearrange("p (a b) -> p a b", a=4), [pb.t], [dst_tok])

    def load_weight_bf16(dst_ap, dst_tok, src_d, c0, c1, stage_bufs, engs=('act', 'pool'), dma_eng='sp'):
        srcv = src_d.rearrange("(k p) n -> p k n", p=128)
        nchunk = (c1 - c0) // 256
        for c in range(nchunk):
            stg = stage_bufs[c % len(stage_bufs)]
            dma(stg.ap, srcv[:, :, c0 + c * 256:c0 + (c + 1) * 256], [], [stg.t], eng=dma_eng)
            tcopy(engs[c % len(engs)], dst_ap[:, :, c * 256:(c + 1) * 256], stg.ap, [stg.t], [dst_tok])

    def load_ln(cv, a, b_):
        g = cv.get([128, 1024], F32)
        bb = cv.get([128, 1024], F32)
        dma(g.ap, lntab_d[a], [], [g.t])
        dma(bb.ap, lntab_d[b_], [], [bb.t])
        return g, bb

    def carve_common(cv):
        KT = cv.get([128, 4, NT * 128], BF16)
        V1 = cv.get([128, NT, 8, 65], BF16)
        QT = cv.get([128, 4, 512], BF16)
        aT = cv.get([128, 4, 512], BF16)
        h0own = cv.get([128, 4, 1024], F32)
        return KT, V1, QT, aT, h0own

    dma(ident.ap, ident_d, [], [ident.t])
    dma(iota128.ap, iota128_d, [], [iota128.t])
    dma(iota16.ap, iota16_d, [], [iota16.t])
    dma(pscale.ap, pscale_d, [], [pscale.t])
    dma(postv.ap, postv_d, [], [postv.t])
    dma(invc.ap, invc_d, [], [invc.t])
    cv = Carver()
    stg = [cv.get([128, 8, 256], F32) for _ in range(2)]
    load_weight_bf16(woutb.ap, woutb.t, w_out_d, 0, 1024, stg)
    wtmp = cv.get([128, 8, 2048], BF16)
    for src_d_, dst_d_ in ((w_in_d, Winb), (wq_d, Wqb)):
        load_weight_bf16(wtmp.ap, wtmp.t, src_d_, 0, 2048, stg)
        for c in range(4):
            dma(dst_d_[:, :, c * 512:(c + 1) * 512], wtmp.ap[:, :, c * 512:(c + 1) * 512], [wtmp.t], [])
    kst = cv.get([128, 16, 128], F32)
    dma(kst.ap, keysT_d, [], [kst.t])
    tcopy('act', keysb.ap, kst.ap, [kst.t], [keysb.t])
    pst = cv.get([128, 4, 128], F32)
    dma(pst.ap, poolw_d.rearrange("g c d -> c g d"), [], [pst.t])
    tcopy('act', poolwb.ap, pst.ap, [pst.t], [poolwb.t])

    S.barrier()
    cv = Carver()
    cvs = [cv.get([128, 4096], F32) for _ in range(4)]
    cvb = [cv.get([128, 4096], BF16) for _ in range(4)]
    ci = 0
    for src, dst in ((UT_d, Ub), (V_d, Vb)):
        for c in range(32):
            a, b_ = cvs[ci % 4], cvb[ci % 4]
            dma(a.ap, src[:, c * 4096:(c + 1) * 4096], [], [a.t])
            tcopy('dve' if ci % 2 else 'act', b_.ap, a.ap, [a.t], [b_.t])
            dma(dst[:, c * 4096:(c + 1) * 4096], b_.ap, [b_.t], [], eng='pool' if ci % 2 else 'sp')
            ci += 1
    S.barrier()

    for seg in range(NSEG):
        cv = Carver()
        KT, V1, QT, aT, h0own = carve_common(cv)
        lng, lnb = load_ln(cv, 0, 1)
        wch = [cv.get([128, 8, 512], BF16) for _ in range(2)]
        wch2 = [cv.get([128, 8, 512], BF16) for _ in range(2)]
        xt = [cv.get([128, 1024], F32) for _ in range(3)]
        h0t = [cv.get([128, 1024], F32) for _ in range(2)]
        scrA = [cv.get([128, 1024], F32) for _ in range(2)]
        h0T = cv.get([128, 8, NT * 128], BF16)
        h0T_ts = [Tok() for _ in range(NT)]
        zf = cv.get([128, 4, 528], F32)
        pa = cv.get([128, 528], F32)
        pbuf = cv.get([128, 528], F32)
        fix = cv.get([128, 8], F32)
        pooled = cv.get([128, 4, 512], BF16)
        S.op('pool', lambda h, V1=V1: h.memset(V1.ap[:, :, :, 64:65], 1.0), [], [V1.t])
        Wk, Wv = wch
        sms = {}

        def a_s1(t):
            x = xt[t % 3]
            dma(x.ap, xin[seg, t], [], [x.t])
            sms[t] = ln_stats(x.ap, [x.t])

        a_s1(0)
        a_s1(1)
        dma(Wk.ap, Winb[:, :, 1024:1536], [], [Wk.t])
        dma(Wv.ap, Winb[:, :, 1536:2048], [], [Wv.t])
        Wq, Wp = wch2
        dma(Wp.ap, Winb[:, :, 0:512], [], [Wp.t])
        dma(Wq.ap, Winb[:, :, 512:1024], [], [Wq.t])
        def a_kv(t):
            pb = psum()
            for p in range(4):
                for k in range(8):
                    mm(pb.ap[:, p * 128:(p + 1) * 128], Wk.ap[:, k, p * 128:(p + 1) * 128],
                       h0T.ap[:, k, t * 128:(t + 1) * 128], k == 0, k == 7, [Wk.t, h0T_ts[t]], [pb.t])
            tcopy('act', KT.ap[:, :, t * 128:(t + 1) * 128], pb.ap[:, 0:512].rearrange("p (a b) -> p a b", a=4),
                  [pb.t], [KT.t])
            pb = psum()
            for k in range(8):
                mm(pb.ap[:, 0:512], h0T.ap[:, k, t * 128:(t + 1) * 128], Wv.ap[:, k, :],
                   k == 0, k == 7, [Wv.t, h0T_ts[t]], [pb.t])
            tcopy('act', V1.ap[:, t, :, 0:64], pb.ap[:, 0:512].rearrange("p (h d) -> p h d", h=8),
                  [pb.t], [V1.t])

        for t in range(NT):
            if t + 2 < NT:
                a_s1(t + 2)
            x = xt[t % 3]
            if 3 <= t < 7:
                ho_ap, ho_tok = h0own.ap[:, t - 3, :], h0own.t
            else:
                ho_ap, ho_tok = h0t[t % 2].ap, h0t[t % 2].t
            ln_apply(sms[t], x.ap, [x.t], ho_ap, ho_tok, lng, lnb, scrA[t % 2])
            if DEBUG_H2 and DEBUG_WHAT == 'h0' and 3 <= t < 7:
                dma(dbg[seg * 512 + (t - 3) * 128: seg * 512 + (t - 2) * 128, :], ho_ap, [ho_tok], [])
            transpose8(ho_ap, ho_tok, lambda k0, k1, t=t: h0T.ap[:, k0:k1, t * 128:(t + 1) * 128], h0T_ts[t])
            if t >= 1:
                a_kv(t - 1)
        a_kv(NT - 1)
        own_ts = h0T_ts[3:7]
        Wq, Wp = wch2
        for g in range(4):
            pb = psum()
            for k in range(8):
                mm(pb.ap[:, 0:512], Wp.ap[:, k, g * 128:(g + 1) * 128], h0T.ap[:, k, 384:896],
                   k == 0, k == 7, [Wp.t] + own_ts, [pb.t])
            for k in range(8):
                mm(pb.ap[:, 512:528], Wp.ap[:, k, g * 128:(g + 1) * 128], h0T.ap[:, k, 16:32],
                   k == 0, k == 7, [Wp.t, h0T_ts[0]], [pb.t])
            tcopy('act', zf.ap[:, g, 8:520], pb.ap[:, 0:512], [pb.t], [zf.t])
            tcopy('act', zf.ap[:, g, 0:8], pb.ap[:, 512:520], [pb.t], [zf.t])
            tt('dve', zf.ap[:, g, 520:528], pb.ap[:, 520:528], postv.ap[:, seg, :], ALU.mult,
               [pb.t, postv.t], [zf.t])
        for p in range(4):
            pb = psum()
            for k in range(8):
                mm(pb.ap[:, 0:512], Wq.ap[:, k, p * 128:(p + 1) * 128], h0T.ap[:, k, 384:896],
                   k == 0, k == 7, [Wq.t] + own_ts, [pb.t])
            tcopy('act', QT.ap[:, p, :], pb.ap[:, 0:512], [pb.t], [QT.t])
        wins = [2, 4, 8, 16]
        for g in range(4):
            w_ = wins[g]
            src = zf.ap[:, g, :]
            cur_tok = zf.t
            n = 528
            step = 1
            bufs = [pa, pbuf]
            bi = 0
            while step < w_:
                dstb = bufs[bi % 2]
                bi += 1
                n2 = n - step
                tt('dve', dstb.ap[:, 0:n2], src[:, 0:n2], src[:, step:step + n2], ALU.add, [cur_tok], [dstb.t])
                src = dstb.ap
                cur_tok = dstb.t
                n = n2
                step *= 2
            off = 8 - w_ // 2
            stt(pooled.ap[:, g, :], src[:, off:off + 512], 1.0 / w_, zf.ap[:, g, 8:520], ALU.mult, ALU.subtract,
                [cur_tok, zf.t], [pooled.t])
            tt('dve', fix.ap, src[:, off + 504:off + 512], invc.ap[:, seg, g, :], ALU.mult,
               [cur_tok, invc.t], [fix.t])
            tt('dve', pooled.ap[:, g, 504:512], fix.ap, zf.ap[:, g, 512:520], ALU.subtract,
               [fix.t, zf.t], [pooled.t])
        for g in range(4):
            pb = psum()
            mm(pb.ap[:, 0:512], poolwb.ap[:, g, :], pooled.ap[:, g, :], True, True, [poolwb.t, pooled.t], [pb.t])
            ts('dve', aT.ap[:, g, :], pb.ap[:, 0:512], pscale.ap[:, g:g + 1], None, ALU.mult, None,
               [pb.t, pscale.t], [aT.t])
        S.barrier()

        cv = Carver()
        KT, V1, QT, aT, h0own = carve_common(cv)
        lng, lnb = load_ln(cv, 2, 3)
        biasb = [cv.get([128, 1024], F32) for _ in range(4)]
        Sb = [cv.get([128, 1024], F32) for _ in range(3)]
        PT = [cv.get([128, 8, 128], BF16) for _ in range(3)]
        bouts = [cv.get([128, 8, 64], F32) for _ in range(4)]
        rden = cv.get([128, 8], F32)
        mixTs = [cv.get([128, 4, 128], BF16) for _ in range(4)]
        pres = [cv.get([128, 1024], F32) for _ in range(4)]
        assert cv.off <= OFF_W, cv.off
        bigW = cv.at(OFF_W, [128, 8, 2048], BF16)
        pbn[0] = 3
        po = PB[3]
        wq_v = wq_d.rearrange("(k p) n -> p k n", p=128)

        def ktile(i, j):
            if j == 5:
                return 0
            d = (-2, -1, 0, 1, 2, None, -3, 3)[j]
            return min(max(i + d, -2), 5) + 3

        def att(i):
            slot = 0
            if seg == 0 and i < 2:
                slot = 1 + i
            elif seg == 3 and i >= 2:
                slot = 1 + i
            elif seg == 4 and i < 2:
                slot = 5 + i
            elif seg == 7 and i >= 2:
                slot = 5 + i
            nj = 8 if (seg, i) in ((0, 0), (3, 3), (4, 0), (7, 3)) else 6

            def emit_S(hd):
                n_ = i * 8 + hd
                bb, sbuf_, pt = biasb[n_ % 4], Sb[n_ % 3], PT[n_ % 3]
                dma(bb.ap.rearrange("p (a b) -> p a b", a=8)[:, 0:nj, :], bias_d[slot, hd][:, 0:nj, :], [], [bb.t])
                ps = psum()
                p_, base = hd // 2, (hd % 2) * 64
                for j in range(nj):
                    kt = ktile(i, j)
                    mm(ps.ap[:, j * 128:(j + 1) * 128], KT.ap[base:base + 64, p_, kt * 128:(kt + 1) * 128],
                       QT.ap[base:base + 64, p_, i * 128:(i + 1) * 128], True, True, [KT.t, QT.t], [ps.t])
                stt(sbuf_.ap[:, 0:nj * 128], ps.ap[:, 0:nj * 128], 0.125, bb.ap[:, 0:nj * 128], ALU.mult, ALU.add,
                    [ps.t, bb.t], [sbuf_.t])
                act(pt.ap.rearrange("p a b -> p (a b)")[:, 0:nj * 128], sbuf_.ap[:, 0:nj * 128], AF.Exp,
                    [sbuf_.t], [pt.t])

            def emit_PV(hd):
                pt = PT[(i * 8 + hd) % 3]
                for j in range(nj):
                    kt = ktile(i, j)
                    mm(po.ap[:, hd * 128:hd * 128 + 65], pt.ap[:, j, :], V1.ap[:, kt, hd, :], j == 0, j == nj - 1,
                       [pt.t, V1.t], [po.t])

            emit_S(0)
            emit_S(1)
            for hd in range(8):
                if hd + 2 < 8:
                    emit_S(hd + 2)
                emit_PV(hd)

        def tail_a(i):
            pov = po.ap.rearrange("p (h d) -> p h d", h=8)
            bout = bouts[i]
            S.op('dve', lambda h: h.reciprocal(out=rden.ap.unsqueeze(2), in_=pov[:, :, 64:65]), [po.t], [rden.t])
            tt('dve', bout.ap, pov[:, :, 0:64], rden.ap.unsqueeze(2).to_broadcast([128, 8, 64]), ALU.mult,
               [po.t, rden.t], [bout.t])

        for i in range(4):
            att(i)
            tail_a(i)
            dma(bigW.ap[:, :, i * 512:(i + 1) * 512], Wqb[:, :, i * 512:(i + 1) * 512], [], [bigW.t])
        pbn[0] = 4
        for i in range(4):
            pb = psum()
            boutf = bouts[i].ap.rearrange("p h d -> p (h d)")
            for kk in range(4):
                tr(pb.ap[:, kk * 128:(kk + 1) * 128], boutf[:, kk * 128:(kk + 1) * 128], [bouts[i].t], [pb.t])
            tcopy('act', mixTs[i].ap, pb.ap[:, 0:512].rearrange("p (a b) -> p a b", a=4), [pb.t], [mixTs[i].t])
        pms = []
        for i in range(4):
            pm = psum()
            pms.append(pm)
            for nh in range(2):
                for k in range(8):
                    if k < 4:
                        lhsT, ltok = aT.ap[:, k, i * 128:(i + 1) * 128], aT.t
                    else:
                        lhsT, ltok = mixTs[i].ap[:, k - 4, :], mixTs[i].t
                    mm(pm.ap[:, nh * 512:(nh + 1) * 512], lhsT, woutb.ap[:, k, nh * 512:(nh + 1) * 512],
                       k == 0, k == 7, [ltok, woutb.t], [pm.t])
        h2_ts = [Tok() for _ in range(4)]
        smB = []
        for i in range(4):
            stt(pres[i].ap, h0own.ap[:, i, :], ALPHA, pms[i].ap, ALU.mult, ALU.add, [h0own.t, pms[i].t], [pres[i].t])
            smB.append(ln_stats(pres[i].ap, [pres[i].t]))
        for i in range(4):
            o_ap = h2s.ap[:, i, :]
            ts('dve', o_ap, pres[i].ap, smB[i].ap[:, 12:13], smB[i].ap[:, 14:15], ALU.subtract, ALU.mult,
               [pres[i].t, smB[i].t], [h2_ts[i]])
            tt('dve', o_ap, o_ap, lng.ap, ALU.mult, [h2_ts[i], lng.t], [h2_ts[i]])
            tt('dve', o_ap, o_ap, lnb.ap, ALU.add, [h2_ts[i], lnb.t], [h2_ts[i], h2s.t])
        for i in range(4):
            transpose8(h2s.ap[:, i, :], h2_ts[i], lambda k0, k1, i=i: h2T.ap[:, k0:k1, i * 128:(i + 1) * 128], h2T.t)
            if DEBUG_H2 and DEBUG_WHAT == 'h2':
                dma(dbg[seg * 512 + i * 128: seg * 512 + (i + 1) * 128, :], h2s.ap[:, i, :], [h2_ts[i]], [])
        pbn[0] = 4
        S.barrier()
        if STOP_AFTER == 'B':
            continue

        cv = Carver(limit=OFF_W)
        bigW = cv.at(OFF_W, [128, 8, 2048], BF16)
        b2 = []
        for _par in range(2):
            b2.append(dict(qT=cv.get([128, 16, 128], BF16), Ssb=cv.get([128, 2048], F32), S2=cv.get([128, 2048], F32),
                           Ssb_ts=[Tok() for _ in range(16)], S2_ts=[Tok() for _ in range(16)]))
        Tv = cv.get([128, 16, 16], F32)
        Ti = cv.get([128, 16, 16], U32)
        Tif = cv.get([128, 16, 16], F32)
        TSv = cv.get([128, 8, 16], F32)
        PI = cv.get([128, 8, 16], U32)
        PIf = cv.get([128, 8, 16], F32)
        Jf = cv.get([128, 8, 16], F32)
        Kf = cv.get([128, 8, 16], F32)
        i1f = cv.get([128, 128], F32)
        i2f = cv.get([128, 128], F32)
        Gt = cv.get([128, 128], F32)
        Zs = cv.get([128, 8], F32)
        Tv4 = Tv.ap.rearrange("p (h two) j -> p h two j", two=2)
        Tif4 = Tif.ap.rearrange("p (h two) j -> p h two j", two=2)
        Gt3 = Gt.ap.rearrange("p (h r) -> p h r", h=8)
        Tv_ts = [Tok() for _ in range(16)]
        Ti_ts = [Tok() for _ in range(16)]
        TSv_ts = [Tok() for _ in range(8)]
        PI_ts = [Tok() for _ in range(8)]

        def b2_front(i):
            B_ = b2[i % 2]
            qT, Ssb, S2, Ssb_ts, S2_ts = B_['qT'], B_['Ssb'], B_['S2'], B_['Ssb_ts'], B_['S2_ts']
            Ssb3 = Ssb.ap.rearrange("p (g n) -> p g n", g=16)
            S23 = S2.ap.rearrange("p (g n) -> p g n", g=16)
            cand4 = Ssb.ap.rearrange("p (h j k) -> p h j k", h=8, j=16)
            candf = Ssb.ap.rearrange("p (h c) -> p h c", h=8)
            cand2f = S2.ap.rearrange("p (h c) -> p h c", h=8)
            eq4 = S2.ap.rearrange("p (h j k) -> p h j k", h=8, j=16)
            for g in range(16):
                pb = psum()
                for k in range(8):
                    mm(pb.ap[:, 0:128], bigW.ap[:, k, g * 128:(g + 1) * 128], h2T.ap[:, k, i * 128:(i + 1) * 128],
                       k == 0, k == 7, [bigW.t, h2T.t], [pb.t])
                tcopy('act', qT.ap[:, g, :], pb.ap[:, 0:128], [pb.t], [qT.t])
            for half in range(2):
                pb = psum()
                for gg in range(8):
                    g = half * 8 + gg
                    mm(pb.ap[:, gg * 128:(gg + 1) * 128], qT.ap[:, g, :], keysb.ap[:, g, :],
                       True, True, [qT.t, keysb.t], [pb.t])
                tcopy('act', Ssb.ap[:, half * 1024:(half + 1) * 1024], pb.ap, [pb.t], Ssb_ts[half * 8:(half + 1) * 8])

        def b2_chain(i):
            B_ = b2[i % 2]
            qT, Ssb, S2, Ssb_ts, S2_ts = B_['qT'], B_['Ssb'], B_['S2'], B_['Ssb_ts'], B_['S2_ts']
            Ssb3 = Ssb.ap.rearrange("p (g n) -> p g n", g=16)
            S23 = S2.ap.rearrange("p (g n) -> p g n", g=16)
            cand4 = Ssb.ap.rearrange("p (h j k) -> p h j k", h=8, j=16)
            candf = Ssb.ap.rearrange("p (h c) -> p h c", h=8)
            cand2f = S2.ap.rearrange("p (h c) -> p h c", h=8)
            eq4 = S2.ap.rearrange("p (h j k) -> p h j k", h=8, j=16)
            for g in range(16):
                S.op('dve', lambda h, g=g: h.max(out=Tv.ap[:, g, 0:8], in_=Ssb3[:, g, :]), [Ssb_ts[g]], [Tv_ts[g]])
            for g in range(16):
                S.op('dve', lambda h, g=g: h.max_index(out=Ti.ap[:, g, 0:8], in_max=Tv.ap[:, g, 0:8],
                                                       in_values=Ssb3[:, g, :]), [Ssb_ts[g], Tv_ts[g]], [Ti_ts[g]])
            for g in range(16):
                S.op('dve', lambda h, g=g: h.match_replace(out=S23[:, g, :], in_to_replace=Tv.ap[:, g, 0:8],
                                                           in_values=Ssb3[:, g, :], imm_value=NEG),
                     [Ssb_ts[g], Tv_ts[g]], [S2_ts[g]])
            for g in range(16):
                S.op('dve', lambda h, g=g: h.max(out=Tv.ap[:, g, 8:16], in_=S23[:, g, :]), [S2_ts[g]], [Tv_ts[g]])
            for g in range(16):
                S.op('dve', lambda h, g=g: h.max_index(out=Ti.ap[:, g, 8:16], in_max=Tv.ap[:, g, 8:16],
                                                       in_values=S23[:, g, :]), [S2_ts[g], Tv_ts[g]], [Ti_ts[g]])
            tcopy('dve', Tif.ap, Ti.ap, Ti_ts, [Tif.t])
            tt('dve', cand4, Tv4[:, :, 0, :].unsqueeze(3).to_broadcast([128, 8, 16, 16]),
               Tv4[:, :, 1, :].unsqueeze(2).to_broadcast([128, 8, 16, 16]), ALU.add, Tv_ts, Ssb_ts)
            hs = lambda L, hh: [L[2 * hh], L[2 * hh + 1]]
            for hh in range(8):
                S.op('dve', lambda h, hh=hh: h.max(out=TSv.ap[:, hh, 0:8], in_=candf[:, hh, :]),
                     hs(Ssb_ts, hh), [TSv_ts[hh]])
            for hh in range(8):
                S.op('dve', lambda h, hh=hh: h.max_index(out=PI.ap[:, hh, 0:8], in_max=TSv.ap[:, hh, 0:8],
                                                         in_values=candf[:, hh, :]),
                     hs(Ssb_ts, hh) + [TSv_ts[hh]], [PI_ts[hh]])
            for hh in range(8):
                S.op('dve', lambda h, hh=hh: h.match_replace(out=cand2f[:, hh, :], in_to_replace=TSv.ap[:, hh, 0:8],
                                                             in_values=candf[:, hh, :], imm_value=NEG),
                     hs(Ssb_ts, hh) + [TSv_ts[hh]], hs(S2_ts, hh))
            for hh in range(8):
                S.op('dve', lambda h, hh=hh: h.max(out=TSv.ap[:, hh, 8:16], in_=cand2f[:, hh, :]),
                     hs(S2_ts, hh), [TSv_ts[hh]])
            for hh in range(8):
                S.op('dve', lambda h, hh=hh: h.max_index(out=PI.ap[:, hh, 8:16], in_max=TSv.ap[:, hh, 8:16],
                                                         in_values=cand2f[:, hh, :]),
                     hs(S2_ts, hh) + [TSv_ts[hh]], [PI_ts[hh]])
            tt('dve', Gt3, TSv.ap, TSv.ap[:, :, 0:1].to_broadcast([128, 8, 16]), ALU.subtract, TSv_ts, [Gt.t])
            act(Gt.ap, Gt.ap, AF.Exp, [Gt.t], [Gt.t])
            S.op('dve', lambda h: h.tensor_reduce(out=Zs.ap, in_=Gt3, axis=AX.X, op=ALU.add), [Gt.t], [Zs.t])
            S.op('dve', lambda h: h.reciprocal(out=Zs.ap, in_=Zs.ap), [Zs.t], [Zs.t])
            tt('dve', Gt3, Gt3, Zs.ap.unsqueeze(2).to_broadcast([128, 8, 16]), ALU.mult, [Gt.t, Zs.t], [Gt.t])
            tcopy('dve', PIf.ap, PI.ap, PI_ts, [PIf.t])
            thr4 = iota16.ap[:, 16:32].unsqueeze(1).unsqueeze(1).to_broadcast([128, 8, 16, 16])
            io4 = iota16.ap[:, 0:16].unsqueeze(1).unsqueeze(1).to_broadcast([128, 8, 16, 16])
            ts('dve', Jf.ap, PIf.ap, 0.0625, -0.46875, ALU.mult, ALU.add, [PIf.t], [Jf.t])
            ts('dve', Jf.ap, Jf.ap, 12582912.0, None, ALU.add, None, [Jf.t], [Jf.t])
            ts('dve', Jf.ap, Jf.ap, -12582912.0, None, ALU.add, None, [Jf.t], [Jf.t])
            stt(Kf.ap, Jf.ap, -16.0, PIf.ap, ALU.mult, ALU.add, [Jf.t, PIf.t], [Kf.t])
            for (Xf, half, dst) in ((Jf, 0, i1f), (Kf, 1, i2f)):
                tt('dve', eq4, Xf.ap.unsqueeze(3).to_broadcast([128, 8, 16, 16]), io4, ALU.is_equal,
                   [Xf.t, iota16.t], S2_ts)
                tt('dve', eq4, eq4, Tif4[:, :, half, :].unsqueeze(2).to_broadcast([128, 8, 16, 16]), ALU.mult,
                   S2_ts + [Tif.t], S2_ts)
                S.op('dve', lambda h, dst=dst: h.tensor_reduce(out=dst.ap.rearrange("p (h r) -> p h r", h=8),
                                                               in_=eq4, axis=AX.X, op=ALU.add), S2_ts, [dst.t])
            pb = psum()
            for n_, srcb in enumerate((i1f, i2f, Gt)):
                tr(pb.ap[:, n_ * 128:(n_ + 1) * 128], srcb.ap, [srcb.t], [pb.t])
            for n_, dstb in enumerate((i1T, i2T, gT)):
                tcopy('act', dstb.ap[:, i * 128:(i + 1) * 128], pb.ap[:, n_ * 128:(n_ + 1) * 128], [pb.t], [dstb.t])

        b2_front(0)
        for i in range(4):
            if i + 1 < 4:
                b2_front(i + 1)
            b2_chain(i)
        S.barrier()

        cv = Carver()
        lng, lnb = load_ln(cv, 4, 5)
        Gbuf = cv.get([128, 256, 128], BF16)
        ubuf = [cv.get([128, 2, 8, 128], BF16) for _ in range(4)]
        vbuf = [cv.get([128, 2, 1024], BF16) for _ in range(4)]
        OH1 = [cv.get([128, 16, 128], BF16) for _ in range(2)]
        OH2 = [cv.get([128, 16, 128], BF16) for _ in range(2)]
        Ag = [cv.get([128, 256], F32) for _ in range(2)]
        AG = [cv.get([128, 256], BF16) for _ in range(2)]
        pre2 = [cv.get([128, 1024], F32) for _ in range(2)]
        yo = [cv.get([128, 1024], F32) for _ in range(2)]
        Ubv = Ub.rearrange("p (j c i) -> p j c i", j=128, c=8)
        Vbv = Vb.rearrange("p (j d) -> p j d", j=128)
        PF = [PB[0], PB[1]]
        PR = [PB[2], PB[3]]
        pri = [0]

        def build(tb):
            t0 = tb * 256
            for c16 in range(16):
                o1, o2 = OH1[c16 % 2], OH2[c16 % 2]
                c0 = t0 + c16 * 16
                for tl in range(16):
                    c = c0 + tl
                    kw1 = dict(w=[o1.t]) if tl == 0 else dict(w_nodep=[o1.t])
                    kw2 = dict(w=[o2.t]) if tl == 0 else dict(w_nodep=[o2.t])
                    S.op('dve', lambda h, o1=o1, tl=tl, c=c: h.tensor_scalar(
                        out=o1.ap[:, tl, :], in0=iota128.ap, scalar1=i1T.ap[:, c:c + 1], scalar2=gT.ap[:, c:c + 1],
                        op0=ALU.is_equal, op1=ALU.mult), [iota128.t, i1T.t, gT.t], **kw1)
                    S.op('dve', lambda h, o2=o2, tl=tl, c=c: h.tensor_scalar(
                        out=o2.ap[:, tl, :], in0=iota128.ap, scalar1=i2T.ap[:, c:c + 1], scalar2=None,
                        op0=ALU.is_equal), [iota128.t, i2T.t], **kw2)
                for q8 in range(2):
                    pr = PR[pri[0] % 2]
                    pri[0] += 1
                    for tk in range(8):
                        tl = q8 * 8 + tk
                        mm(pr.ap[:, tk * 128:(tk + 1) * 128], o1.ap[:, tl, :], o2.ap[:, tl, :], True, True,
                           [o1.t, o2.t], [pr.t])
                    tl0 = c16 * 16 + q8 * 8
                    tcopy('act', Gbuf.ap[:, tl0:tl0 + 8, :], pr.ap.rearrange("p (a b) -> p a b", a=8),
                          [pr.t], [Gbuf.t])

        def load_grp(gg):
            ub, vb = ubuf[gg % 4], vbuf[gg % 4]
            dma(ub.ap, Ubv[:, gg * 2:(gg + 1) * 2, :, :], [], [ub.t], eng='sp')
            dma(vb.ap, Vbv[:, gg * 2:(gg + 1) * 2, :], [], [vb.t], eng='sp')

        def sweep(tb, hook):
            t0 = tb * 256

            def emit_AT(j):
                gg, jj = j // 2, j % 2
                ub = ubuf[gg % 4]
                pr = PR[j % 2]
                ag, agb = Ag[j % 2], AG[j % 2]
                for k in range(8):
                    mm(pr.ap[:, 0:256], ub.ap[:, jj, k, :], h2T.ap[:, k, t0:t0 + 256], k == 0, k == 7,
                       [ub.t, h2T.t], [pr.t])
                act(ag.ap, pr.ap[:, 0:256], AF.Gelu, [pr.t], [ag.t])
                tt('dve', agb.ap, ag.ap, Gbuf.ap[:, :, j], ALU.mult, [ag.t, Gbuf.t], [agb.t])

            def emit_F(j):
                gg, jj = j // 2, j % 2
                vb = vbuf[gg % 4]
                agb = AG[j % 2]
                for tt_ in range(2):
                    for nh in range(2):
                        mm(PF[tt_].ap[:, nh * 512:(nh + 1) * 512], agb.ap[:, tt_ * 128:(tt_ + 1) * 128],
                           vb.ap[:, jj, nh * 512:(nh + 1) * 512], j == 0, j == 127,
                           [agb.t, vb.t], [PF[tt_].t])

            for gg in range(4):
                load_grp(gg)
            emit_AT(0)
            for j in range(128):
                if j + 1 < 128:
                    emit_AT(j + 1)
                emit_F(j)
                if j % 2 == 1 and j // 2 + 4 < 64:
                    load_grp(j // 2 + 4)
                if j == 12 and hook is not None:
                    hook()

        def fin_pre(tb):
            for tt_ in range(2):
                i = tb * 2 + tt_
                stt(pre2[tt_].ap, h2s.ap[:, i, :], ALPHA, PF[tt_].ap, ALU.mult, ALU.add,
                    [h2s.t, PF[tt_].t], [pre2[tt_].t])

        def fin_ln(tb):
            for tt_ in range(2):
                i = tb * 2 + tt_
                yb = yo[tt_]
                sm = ln_stats(pre2[tt_].ap, [pre2[tt_].t])
                ts('dve', yb.ap, pre2[tt_].ap, sm.ap[:, 12:13], sm.ap[:, 14:15], ALU.subtract, ALU.mult,
                   [pre2[tt_].t, sm.t], [yb.t])
                tt('dve', yb.ap, yb.ap, lng.ap, ALU.mult, [yb.t, lng.t], [yb.t])
                tt('dve', yb.ap, yb.ap, lnb.ap, ALU.add, [yb.t, lnb.t], [yb.t])
                r0 = seg * 512 + i * 128
                dma(y[r0:r0 + 128, :], yb.ap, [yb.t], [])

        build(0)
        sweep(0, None)
        build(1)
        fin_pre(0)
        sweep(1, lambda: fin_ln(0))
        fin_pre(1)
        fin_ln(1)
        S.barrier()

    S.emit(nc, es)
    es.close()
    return nc


def _bias_table(rpb, rq0, R):
    tab = np.full((8, 128, 8, 128), NEG, np.float32)
    q = np.arange(128)
    rq = rq0 + q // 64
    cq = q % 64
    rs = np.clip(rq - 4, 0, R - 8)
    cs = np.clip(cq - 8, 0, 48)
    k = np.arange(128)
    ck = k % 64
    order = {-2: 0, -1: 1, 0: 2, 1: 3, 2: 4, -3: 6, 3: 7}
    for dlt, j in order.items():
        rk = rq0 + 2 * dlt + k // 64
        row_ok = (rk[:, None] >= rs[None, :]) & (rk[:, None] < rs[None, :] + 8) & (rk[:, None] >= 0) & (rk[:, None] < R)
        col_ok = (ck[:, None] >= cs[None, :]) & (ck[:, None] < cs[None, :] + 16)
        ok = row_ok & col_ok
        ro = np.clip(rk[:, None] - rq[None, :] + 7, 0, 14)
        co = np.clip(ck[:, None] - cq[None, :] + 15, 0, 30)
        vals = rpb[:, ro, co]
        tab[:, :, j, :] = np.where(ok[None], vals, np.float32(NEG))
    tab[:, 0:16, 5, :] = 0.0
    return tab


def _seg_tokens(xseq, meta, start):
    T, D = xseq.shape
    out = np.zeros((NT, 128, D), np.float32)
    out[0, 0:16] = meta
    if start == 0:
        out[0, 16:24] = meta[8:16]
    else:
        out[0, 16:24] = xseq[start - 8:start]
    e = start + 512
    n_post = min(8, T - e)
    if n_post > 0:
        out[0, 24:24 + n_post] = xseq[e:e + n_post]
    flat = out[1:].reshape(8 * 128, D)
    lo = start - 256
    hi = start + 512 + 256
    a, b = max(lo, 0), min(hi, T)
    flat[a - lo:b - lo] = xseq[a:b]
    return out, n_post


def _prepare(inputs):
    f = lambda k: np.asarray(inputs[k], np.float32)
    xp, xs, meta = f("x_prompt"), f("x_sample"), f("meta_tokens")
    rpb = f("nat_rpb")[0]
    shared = {}
    shared["w_in"] = np.ascontiguousarray(f("w_in")[0])
    shared["w_out"] = np.ascontiguousarray(f("w_out")[0])
    shared["wq"] = np.ascontiguousarray(f("peer_wq")[0])
    k1, k2 = f("peer_key1")[0], f("peer_key2")[0]
    keys = np.stack([k1, k2], axis=1).reshape(16, 128, 128)
    shared["keysT"] = np.ascontiguousarray(keys.transpose(2, 0, 1))
    shared["pool_w"] = np.ascontiguousarray(f("pool_w")[0])
    shared["pscale"] = np.ascontiguousarray(f("pool_scale")[0].reshape(4, 128).T)
    lt = np.stack([f("emb_ln_g"), f("emb_ln_b"), f("ln1_g")[0], f("ln1_b")[0], f("ln2_g")[0], f("ln2_b")[0]])
    shared["lntab"] = np.ascontiguousarray(np.broadcast_to(lt[:, None, :], (6, 128, 1024)))
    shared["ident"] = np.eye(128, dtype=np.float32)
    shared["iota128"] = np.broadcast_to(np.arange(128, dtype=np.float32)[None], (128, 128)).astype(ml_dtypes.bfloat16)
    io = np.concatenate([np.arange(16, dtype=np.float32), 16.0 * (np.arange(16, dtype=np.float32) + 1.0)])
    shared["iota16"] = np.ascontiguousarray(np.broadcast_to(io[None], (128, 32)))
    u, v = f("peer_u")[0], f("peer_v")[0]
    shared["UT"] = np.ascontiguousarray(u.reshape(128, 128, 8, 128).transpose(3, 1, 2, 0)).reshape(128, 131072)
    shared["Vt"] = np.ascontiguousarray(v.reshape(128, 131072))
    wins = [2, 4, 8, 16]
    in_maps = []
    for c in range(8):
        m = dict(shared)
        s_, qtr = c // 4, c % 4
        xin = np.zeros((NSEG, NT, 128, 1024), np.float32)
        postv = np.zeros((128, NSEG, 8), np.float32)
        invc = np.zeros((128, NSEG, 4, 8), np.float32)
        tabs = np.zeros((NSLOT, 8, 128, 8, 128), np.float32)
        tabs[0] = _bias_table(rpb, 8, 32)
        for seg in range(NSEG):
            if seg < 4:
                xseq, start, R = xp[c], seg * 512, 32
            else:
                xseq, start, R = xs[s_], qtr * 2048 + (seg - 4) * 512, 128
            T = xseq.shape[0]
            xin[seg], n_post = _seg_tokens(xseq, meta, start)
            postv[:, seg, :n_post] = 1.0
            L = 16 + T
            for g, w_ in enumerate(wins):
                pos = 16 + start + 504 + np.arange(8)
                lo = np.clip(pos - w_ // 2, 0, L - 1)
                hi = np.clip(pos - w_ // 2 + w_ - 1, 0, L - 1)
                invc[:, seg, g, :] = (1.0 / (hi - lo + 1).astype(np.float32))[None, :]
            row0 = start // 64
            for i in range(4):
                slot = 0
                if seg == 0 and i < 2:
                    slot = 1 + i
                elif seg == 3 and i >= 2:
                    slot = 1 + i
                elif seg == 4 and i < 2:
                    slot = 5 + i
                elif seg == 7 and i >= 2:
                    slot = 5 + i
                if slot:
                    tabs[slot] = _bias_table(rpb, row0 + 2 * i, R)
        m["xin"] = xin
        m["postv"] = postv
        m["invc"] = invc
        m["biastab"] = tabs
        in_maps.append(m)
    return in_maps


_NC_CACHE = {}


def kernel(**inputs):
    in_maps = _prepare(inputs)
    if "nc" not in _NC_CACHE:
        _NC_CACHE["nc"] = build_program()
    nc = _NC_CACHE["nc"]
    res = run_bass_kernel_spmd(nc, in_maps, core_ids=list(range(8)))
    ys = [np.asarray(r["y"], np.float32) for r in res.results]
    y_prompt = np.stack([ys[c][0:2048] for c in range(8)], axis=0)
    y_sample = np.stack([np.concatenate([ys[s * 4 + q][2048:4096] for q in range(4)], axis=0) for s in range(2)], axis=0)
    if DEBUG_H2:
        kernel.dbg = [np.asarray(r["dbg"], np.float32) for r in res.results]
    return (y_prompt, y_sample)
```

```python
import numpy as np
import ml_dtypes
from contextlib import ExitStack
import concourse.bass as bass
import concourse.mybir as mybir
from concourse.bass_utils import run_bass_kernel_spmd

F32 = mybir.dt.float32
BF16 = mybir.dt.bfloat16
U32 = mybir.dt.uint32
U8 = mybir.dt.uint8
ALU = mybir.AluOpType
AF = mybir.ActivationFunctionType
AX = mybir.AxisListType

ALPHA = float(2.0 ** 0.25)
EPS = 1e-5
NEG = -1e30
NSEG = 8
NT = 9
NSLOT = 9
ENG = ['pe', 'act', 'dve', 'pool', 'sp']
DEBUG_H2 = False
DEBUG_WHAT = 'h2'
STOP_AFTER = None


class Tok:
    __slots__ = ('w', 'r')

    def __init__(self):
        self.w = None
        self.r = {}


class Sch:
    def __init__(self, ndma=16):
        self.q = {e: [] for e in ENG}
        self.n = {e: 0 for e in ENG}
        self.dcount = [0] * ndma
        self.dnext = 0
        self.pnext = 0

    def _deps(self, eng, r, w):
        need = {}

        def add(k, i):
            if k == 'pe' and eng == 'pe':
                return
            if need.get(k, 0) < i:
                need[k] = i
        for t in r:
            if t.w is not None:
                add(*t.w)
        for t in w:
            if t.w is not None:
                add(*t.w)
            for k, i in t.r.items():
                add(k, i)
        return need

    def op(self, eng, fn, r=(), w=(), w_nodep=()):
        need = self._deps(eng, r, w)
        self.n[eng] += 1
        idx = self.n[eng]
        self.q[eng].append(dict(waits=need, fn=fn, kind='c', idx=idx))
        for t in r:
            t.r[eng] = idx
        for t in w:
            t.w = (eng, idx)
            t.r = {}
        for t in w_nodep:
            t.w = (eng, idx)

    def dma(self, eng, fn, r=(), w=()):
        need = self._deps(eng, r, w)
        if eng == 'pool':
            d = 12 + self.pnext
            self.pnext = (self.pnext + 1) % 4
        else:
            d = self.dnext
            self.dnext = (self.dnext + 1) % 12
        key = ('d', d)
        if self.dcount[d] > 0:
            need[key] = max(need.get(key, 0), self.dcount[d])
        self.dcount[d] += 16
        val = self.dcount[d]
        self.q[eng].append(dict(waits=need, fn=fn, kind='d', dsem=d))
        for t in r:
            t.r[key] = val
        for t in w:
            t.w = (key, val)
            t.r = {}

    def barrier(self):
        state = {e: self.n[e] for e in ENG if self.n[e] > 0}
        for d, c in enumerate(self.dcount):
            if c > 0:
                state[('d', d)] = c
        for e in ENG:
            need = {k: v for k, v in state.items() if k != e}
            self.q[e].append(dict(waits=need, fn=None, kind='w'))

    def emit(self, nc, es):
        sems = {e: es.enter_context(nc.semaphore("s_" + e)) for e in ENG}
        dsems = [es.enter_context(nc.semaphore("d_%d" % i)) for i in range(len(self.dcount))]
        fin = {('d', d): c for d, c in enumerate(self.dcount) if c > 0}
        self.q['sp'].append(dict(waits=fin, fn=None, kind='w'))
        waited = {e: set() for e in ENG}
        for e in ENG:
            for it in self.q[e]:
                for k, i in it['waits'].items():
                    if isinstance(k, str):
                        waited[k].add(i)
        ms = {e: {i: m + 1 for m, i in enumerate(sorted(waited[e]))} for e in ENG}
        q = self.q

        def run(e, h):
            seen = {}
            for it in q[e]:
                for k, i in it['waits'].items():
                    if isinstance(k, str):
                        sem, val = sems[k], ms[k][i]
                    else:
                        sem, val = dsems[k[1]], i
                    if seen.get(k, 0) >= val:
                        continue
                    seen[k] = val
                    h.wait_ge(sem, val)
                if it['kind'] == 'w':
                    continue
                ins = it['fn'](h)
                if it['kind'] == 'd':
                    ins.then_inc(dsems[it['dsem']], 16)
                elif it['idx'] in ms[e]:
                    ins.then_inc(sems[e], 1)

        with nc.Block() as block:
            @block.tensor
            def _(h):
                run('pe', h)

            @block.scalar
            def _(h):
                run('act', h)

            @block.vector
            def _(h):
                run('dve', h)

            @block.gpsimd
            def _(h):
                run('pool', h)

            @block.sync
            def _(h):
                run('sp', h)


class Buf:
    def __init__(self, ap, is_handle=False):
        self.ap = ap[:] if is_handle else ap
        self.t = Tok()


def build_program():
    nc = bass.Bass("TRN2", target_bir_lowering=False)
    S = Sch()

    def din(name, shape, dt=F32):
        return nc.dram_tensor(name, list(shape), dt, kind="ExternalInput").ap()

    xin = din("xin", [NSEG, NT, 128, 1024])
    w_in_d = din("w_in", [1024, 2048])
    w_out_d = din("w_out", [1024, 1024])
    wq_d = din("wq", [1024, 2048])
    keysT_d = din("keysT", [128, 16, 128])
    poolw_d = din("pool_w", [4, 128, 128])
    pscale_d = din("pscale", [128, 4])
    lntab_d = din("lntab", [6, 128, 1024])
    bias_d = din("biastab", [NSLOT, 8, 128, 8, 128])
    postv_d = din("postv", [128, NSEG, 8])
    invc_d = din("invc", [128, NSEG, 4, 8])
    ident_d = din("ident", [128, 128])
    iota128_d = din("iota128", [128, 128], BF16)
    iota16_d = din("iota16", [128, 32])
    iotar_d = din("iotarep", [128, 2048], BF16)
    UT_d = din("UT", [128, 131072])
    V_d = din("Vt", [128, 131072])
    y = nc.dram_tensor("y", [4096, 1024], F32, kind="ExternalOutput").ap()
    if DEBUG_H2:
        dbg = nc.dram_tensor("dbg", [4096, 1024], F32, kind="ExternalOutput").ap()
    Ub = nc.dram_tensor("Ub", [128, 131072], BF16, kind="Internal").ap()
    Vb = nc.dram_tensor("Vb", [128, 131072], BF16, kind="Internal").ap()
    Winb = nc.dram_tensor("Winb", [128, 8, 2048], BF16, kind="Internal").ap()
    Wqb = nc.dram_tensor("Wqb", [128, 8, 2048], BF16, kind="Internal").ap()

    es = ExitStack()

    def sb(name, shape, dt):
        return Buf(es.enter_context(nc.sbuf_tensor("sb_" + name, list(shape), dt)), True)

    ident = sb("ident", [128, 128], F32)
    iota128 = sb("iota128", [128, 128], BF16)
    iota16 = sb("iota16", [128, 32], F32)
    iotaR = sb("iotaR", [128, 128, 16], BF16)
    pscale = sb("pscale", [128, 4], F32)
    postv = sb("postv", [128, NSEG, 8], F32)
    invc = sb("invc", [128, NSEG, 4, 8], F32)
    woutb = sb("woutb", [128, 8, 1024], BF16)
    keysb = sb("keysb", [128, 16, 128], BF16)
    poolwb = sb("poolwb", [128, 4, 128], BF16)
    h2s = sb("h2s", [128, 4, 1024], F32)
    h2T = sb("h2T", [128, 8, 512], BF16)
    i1T = sb("i1T", [128, 512], BF16)
    i2T = sb("i2T", [128, 512], BF16)
    gT = sb("gT", [128, 512], BF16)
    small = [sb("small%d" % i, [128, 16], F32) for i in range(4)]
    OVB = 151 * 1024
    OFF_W = 116 * 1024
    OV = es.enter_context(nc.sbuf_tensor("OV", [128, OVB], U8))
    PB = [Buf(es.enter_context(nc.psum_tensor("pb%d" % i, [128, 1024], F32)), True) for i in range(4)]
    pbi = [0]
    pbn = [4]

    def psum():
        b = PB[pbi[0] % pbn[0]]
        pbi[0] += 1
        return b

    class Carver:
        def __init__(self, limit=None):
            self.off = 0
            self.limit = OVB if limit is None else limit

        def at(self, off, shape, dt):
            save = (self.off, self.limit)
            self.off, self.limit = off, OVB
            b = self.get(shape, dt)
            self.off, self.limit = save
            return b

        def get(self, shape, dt):
            sz = {F32: 4, BF16: 2, U32: 4, U8: 1}[dt]
            n = int(np.prod(shape[1:]))
            nb = n * sz
            off = (self.off + 31) // 32 * 32
            assert off + nb <= self.limit, (off, nb, self.limit)
            self.off = off + nb
            v = OV[:, off:off + nb]
            if dt != U8:
                v = v.bitcast(dt)
            if len(shape) == 3:
                v = v.rearrange("p (a b) -> p a b", a=shape[1])
            elif len(shape) == 4:
                v = v.rearrange("p (a b c) -> p a b c", a=shape[1], b=shape[2])
            return Buf(v)

    def dma(out, in_, r=(), w=(), eng='sp'):
        S.dma(eng, lambda h: h.dma_start(out=out, in_=in_), r, w)

    def tcopy(eng, out, in_, r, w):
        if eng == 'act':
            S.op('act', lambda h: h.copy(out=out, in_=in_), r, w)
        else:
            S.op(eng, lambda h: h.tensor_copy(out=out, in_=in_), r, w)

    def tt(eng, out, a, b, op, r, w):
        S.op(eng, lambda h: h.tensor_tensor(out=out, in0=a, in1=b, op=op), r, w)

    def ts(eng, out, a, s1, s2, op0, op1, r, w):
        if s2 is None:
            S.op(eng, lambda h: h.tensor_scalar(out=out, in0=a, scalar1=s1, scalar2=None, op0=op0), r, w)
        else:
            S.op(eng, lambda h: h.tensor_scalar(out=out, in0=a, scalar1=s1, scalar2=s2, op0=op0, op1=op1), r, w)

    def stt(out, a, s, b, op0, op1, r, w):
        S.op('dve', lambda h: h.scalar_tensor_tensor(out=out, in0=a, scalar=s, in1=b, op0=op0, op1=op1), r, w)

    def act(out, in_, func, r, w, scale=None):
        if scale is None:
            S.op('act', lambda h: h.activation(out=out, in_=in_, func=func), r, w)
        else:
            S.op('act', lambda h: h.activation(out=out, in_=in_, func=func, scale=scale), r, w)

    def mm(out, lhsT, rhs, start, stop, r, w):
        S.op('pe', lambda h: h.matmul(out, lhsT, rhs, start=start, stop=stop), r, w)

    def tr(out, in_, r, w):
        S.op('pe', lambda h: h.transpose(out, in_, ident.ap), list(r) + [ident.t], w)

    smi = [0]

    def ln_stats(x_ap, x_toks):
        sm = small[smi[0] % 4]
        smi[0] += 1
        S.op('dve', lambda h: h.bn_stats(out=sm.ap[:, 0:6], in_=x_ap[:, 0:512]), x_toks, [sm.t])
        S.op('dve', lambda h: h.bn_stats(out=sm.ap[:, 6:12], in_=x_ap[:, 512:1024]), x_toks, [sm.t])
        S.op('dve', lambda h: h.bn_aggr(out=sm.ap[:, 12:14], in_=sm.ap[:, 0:12]), [sm.t], [sm.t])
        ts('dve', sm.ap[:, 14:15], sm.ap[:, 13:14], EPS, None, ALU.add, None, [sm.t], [sm.t])
        act(sm.ap[:, 15:16], sm.ap[:, 14:15], AF.Ln, [sm.t], [sm.t])
        act(sm.ap[:, 14:15], sm.ap[:, 15:16], AF.Exp, [sm.t], [sm.t], scale=-0.5)
        return sm

    def ln_apply(sm, x_ap, x_toks, out_ap, out_tok, g, b, scr):
        ts('dve', scr.ap, x_ap, sm.ap[:, 12:13], sm.ap[:, 14:15], ALU.subtract, ALU.mult,
           list(x_toks) + [sm.t], [scr.t])
        tt('dve', scr.ap, scr.ap, g.ap, ALU.mult, [scr.t, g.t], [scr.t])
        tt('dve', out_ap, scr.ap, b.ap, ALU.add, [scr.t, b.t], [out_tok])

    def layer_norm(x_ap, x_toks, out_ap, out_tok, g, b, scr):
        sm = ln_stats(x_ap, x_toks)
        ln_apply(sm, x_ap, x_toks, out_ap, out_tok, g, b, scr)

    def transpose8(src_ap, src_tok, dst_fn, dst_tok):
        for half in range(2):
            pb = psum()
            for kk in range(4):
                k = half * 4 + kk
                tr(pb.ap[:, kk * 128:(kk + 1) * 128], src_ap[:, k * 128:(k + 1) * 128], [src_tok], [pb.t])
            tcopy('act', dst_fn(half * 4, half * 4 + 4),
                  pb.ap[:, 0:512].rearrange("p (a b) -> p a b", a=4), [pb.t], [dst_tok])

    def load_weight_bf16(dst_ap, dst_tok, src_d, c0, c1, stage_bufs, engs=('act', 'pool'), dma_eng='sp'):
        srcv = src_d.rearrange("(k p) n -> p k n", p=128)
        nchunk = (c1 - c0) // 256
        for c in range(nchunk):
            stg = stage_bufs[c % len(stage_bufs)]
            dma(stg.ap, srcv[:, :, c0 + c * 256:c0 + (c + 1) * 256], [], [stg.t], eng=dma_eng)
            tcopy(engs[c % len(engs)], dst_ap[:, :, c * 256:(c + 1) * 256], stg.ap, [stg.t], [dst_tok])

    def load_ln(cv, a, b_):
        g = cv.get([128, 1024], F32)
        bb = cv.get([128, 1024], F32)
        dma(g.ap, lntab_d[a], [], [g.t])
        dma(bb.ap, lntab_d[b_], [], [bb.t])
        return g, bb

    def carve_common(cv):
        KT = cv.get([128, 4, NT * 128], BF16)
        V1 = cv.get([128, NT, 8, 65], BF16)
        QT = cv.get([128, 4, 512], BF16)
        aT = cv.get([128, 4, 512], BF16)
        h0own = cv.get([128, 4, 1024], F32)
        return KT, V1, QT, aT, h0own

    dma(ident.ap, ident_d, [], [ident.t])
    dma(iota128.ap, iota128_d, [], [iota128.t])
    dma(iota16.ap, iota16_d, [], [iota16.t])
    dma(iotaR.ap, iotar_d.rearrange("p (i t) -> p i t", t=16), [], [iotaR.t])
    dma(pscale.ap, pscale_d, [], [pscale.t])
    dma(postv.ap, postv_d, [], [postv.t])
    dma(invc.ap, invc_d, [], [invc.t])
    cv = Carver()
    stg = [cv.get([128, 8, 256], F32) for _ in range(2)]
    load_weight_bf16(woutb.ap, woutb.t, w_out_d, 0, 1024, stg)
    wtmp = cv.get([128, 8, 2048], BF16)
    for src_d_, dst_d_ in ((w_in_d, Winb), (wq_d, Wqb)):
        load_weight_bf16(wtmp.ap, wtmp.t, src_d_, 0, 2048, stg)
        for c in range(4):
            dma(dst_d_[:, :, c * 512:(c + 1) * 512], wtmp.ap[:, :, c * 512:(c + 1) * 512], [wtmp.t], [])
    kst = cv.get([128, 16, 128], F32)
    dma(kst.ap, keysT_d, [], [kst.t])
    tcopy('act', keysb.ap, kst.ap, [kst.t], [keysb.t])
    pst = cv.get([128, 4, 128], F32)
    dma(pst.ap, poolw_d.rearrange("g c d -> c g d"), [], [pst.t])
    tcopy('act', poolwb.ap, pst.ap, [pst.t], [poolwb.t])

    S.barrier()
    cv = Carver()
    cvs = [cv.get([128, 4096], F32) for _ in range(4)]
    cvb = [cv.get([128, 4096], BF16) for _ in range(4)]
    ci = 0
    for src, dst in ((UT_d, Ub), (V_d, Vb)):
        for c in range(32):
            a, b_ = cvs[ci % 4], cvb[ci % 4]
            dma(a.ap, src[:, c * 4096:(c + 1) * 4096], [], [a.t])
            tcopy('dve' if ci % 2 else 'act', b_.ap, a.ap, [a.t], [b_.t])
            dma(dst[:, c * 4096:(c + 1) * 4096], b_.ap, [b_.t], [], eng='pool' if ci % 2 else 'sp')
            ci += 1
    S.barrier()

    for seg in range(NSEG):
        cv = Carver()
        KT, V1, QT, aT, h0own = carve_common(cv)
        lng, lnb = load_ln(cv, 0, 1)
        wch = [cv.get([128, 8, 512], BF16) for _ in range(2)]
        wch2 = [cv.get([128, 8, 512], BF16) for _ in range(2)]
        xt = [cv.get([128, 1024], F32) for _ in range(3)]
        h0t = [cv.get([128, 1024], F32) for _ in range(2)]
        scrA = [cv.get([128, 1024], F32) for _ in range(2)]
        h0T = cv.get([128, 8, NT * 128], BF16)
        h0T_ts = [Tok() for _ in range(NT)]
        zf = cv.get([128, 4, 528], F32)
        pa = cv.get([128, 528], F32)
        pbuf = cv.get([128, 528], F32)
        fix = cv.get([128, 8], F32)
        pooled = cv.get([128, 4, 512], BF16)
        S.op('pool', lambda h, V1=V1: h.memset(V1.ap[:, :, :, 64:65], 1.0), [], [V1.t])
        Wk, Wv = wch
        sms = {}

        def a_s1(t):
            x = xt[t % 3]
            dma(x.ap, xin[seg, t], [], [x.t])
            sms[t] = ln_stats(x.ap, [x.t])

        a_s1(0)
        a_s1(1)
        dma(Wk.ap, Winb[:, :, 1024:1536], [], [Wk.t])
        dma(Wv.ap, Winb[:, :, 1536:2048], [], [Wv.t])
        Wq, Wp = wch2
        dma(Wp.ap, Winb[:, :, 0:512], [], [Wp.t])
        dma(Wq.ap, Winb[:, :, 512:1024], [], [Wq.t])
        def a_kv(t):
            pb = psum()
            for p in range(4):
                for k in range(8):
                    mm(pb.ap[:, p * 128:(p + 1) * 128], Wk.ap[:, k, p * 128:(p + 1) * 128],
                       h0T.ap[:, k, t * 128:(t + 1) * 128], k == 0, k == 7, [Wk.t, h0T_ts[t]], [pb.t])
            tcopy('act', KT.ap[:, :, t * 128:(t + 1) * 128], pb.ap[:, 0:512].rearrange("p (a b) -> p a b", a=4),
                  [pb.t], [KT.t])
            pb = psum()
            for k in range(8):
                mm(pb.ap[:, 0:512], h0T.ap[:, k, t * 128:(t + 1) * 128], Wv.ap[:, k, :],
                   k == 0, k == 7, [Wv.t, h0T_ts[t]], [pb.t])
            tcopy('act', V1.ap[:, t, :, 0:64], pb.ap[:, 0:512].rearrange("p (h d) -> p h d", h=8),
                  [pb.t], [V1.t])

        for t in range(NT):
            if t + 2 < NT:
                a_s1(t + 2)
            x = xt[t % 3]
            if 3 <= t < 7:
                ho_ap, ho_tok = h0own.ap[:, t - 3, :], h0own.t
            else:
                ho_ap, ho_tok = h0t[t % 2].ap, h0t[t % 2].t
            ln_apply(sms[t], x.ap, [x.t], ho_ap, ho_tok, lng, lnb, scrA[t % 2])
            if DEBUG_H2 and DEBUG_WHAT == 'h0' and 3 <= t < 7:
                dma(dbg[seg * 512 + (t - 3) * 128: seg * 512 + (t - 2) * 128, :], ho_ap, [ho_tok], [])
            transpose8(ho_ap, ho_tok, lambda k0, k1, t=t: h0T.ap[:, k0:k1, t * 128:(t + 1) * 128], h0T_ts[t])
            if t >= 1:
                a_kv(t - 1)
        a_kv(NT - 1)
        own_ts = h0T_ts[3:7]
        Wq, Wp = wch2
        for g in range(4):
            pb = psum()
            for k in range(8):
                mm(pb.ap[:, 0:512], Wp.ap[:, k, g * 128:(g + 1) * 128], h0T.ap[:, k, 384:896],
                   k == 0, k == 7, [Wp.t] + own_ts, [pb.t])
            for k in range(8):
                mm(pb.ap[:, 512:528], Wp.ap[:, k, g * 128:(g + 1) * 128], h0T.ap[:, k, 16:32],
                   k == 0, k == 7, [Wp.t, h0T_ts[0]], [pb.t])
            tcopy('act', zf.ap[:, g, 8:520], pb.ap[:, 0:512], [pb.t], [zf.t])
            tcopy('act', zf.ap[:, g, 0:8], pb.ap[:, 512:520], [pb.t], [zf.t])
            tt('dve', zf.ap[:, g, 520:528], pb.ap[:, 520:528], postv.ap[:, seg, :], ALU.mult,
               [pb.t, postv.t], [zf.t])
        for p in range(4):
            pb = psum()
            for k in range(8):
                mm(pb.ap[:, 0:512], Wq.ap[:, k, p * 128:(p + 1) * 128], h0T.ap[:, k, 384:896],
                   k == 0, k == 7, [Wq.t] + own_ts, [pb.t])
            tcopy('act', QT.ap[:, p, :], pb.ap[:, 0:512], [pb.t], [QT.t])
        wins = [2, 4, 8, 16]
        for g in range(4):
            w_ = wins[g]
            src = zf.ap[:, g, :]
            cur_tok = zf.t
            n = 528
            step = 1
            bufs = [pa, pbuf]
            bi = 0
            while step < w_:
                dstb = bufs[bi % 2]
                bi += 1
                n2 = n - step
                tt('dve', dstb.ap[:, 0:n2], src[:, 0:n2], src[:, step:step + n2], ALU.add, [cur_tok], [dstb.t])
                src = dstb.ap
                cur_tok = dstb.t
                n = n2
                step *= 2
            off = 8 - w_ // 2
            stt(pooled.ap[:, g, :], src[:, off:off + 512], 1.0 / w_, zf.ap[:, g, 8:520], ALU.mult, ALU.subtract,
                [cur_tok, zf.t], [pooled.t])
            tt('dve', fix.ap, src[:, off + 504:off + 512], invc.ap[:, seg, g, :], ALU.mult,
               [cur_tok, invc.t], [fix.t])
            tt('dve', pooled.ap[:, g, 504:512], fix.ap, zf.ap[:, g, 512:520], ALU.subtract,
               [fix.t, zf.t], [pooled.t])
        for g in range(4):
            pb = psum()
            mm(pb.ap[:, 0:512], poolwb.ap[:, g, :], pooled.ap[:, g, :], True, True, [poolwb.t, pooled.t], [pb.t])
            ts('dve', aT.ap[:, g, :], pb.ap[:, 0:512], pscale.ap[:, g:g + 1], None, ALU.mult, None,
               [pb.t, pscale.t], [aT.t])
        S.barrier()

        cv = Carver()
        KT, V1, QT, aT, h0own = carve_common(cv)
        lng, lnb = load_ln(cv, 2, 3)
        biasb = [cv.get([128, 1024], F32) for _ in range(4)]
        Sb = [cv.get([128, 1024], F32) for _ in range(3)]
        PT = [cv.get([128, 8, 128], BF16) for _ in range(3)]
        bouts = [cv.get([128, 8, 64], F32) for _ in range(4)]
        rden = cv.get([128, 8], F32)
        mixTs = [cv.get([128, 4, 128], BF16) for _ in range(4)]
        pres = [cv.get([128, 1024], F32) for _ in range(4)]
        assert cv.off <= OFF_W, cv.off
        bigW = cv.at(OFF_W, [128, 8, 2048], BF16)
        pbn[0] = 3
        po = PB[3]
        wq_v = wq_d.rearrange("(k p) n -> p k n", p=128)

        def ktile(i, j):
            if j == 5:
                return 0
            d = (-2, -1, 0, 1, 2, None, -3, 3)[j]
            return min(max(i + d, -2), 5) + 3

        def att(i):
            slot = 0
            if seg == 0 and i < 2:
                slot = 1 + i
            elif seg == 3 and i >= 2:
                slot = 1 + i
            elif seg == 4 and i < 2:
                slot = 5 + i
            elif seg == 7 and i >= 2:
                slot = 5 + i
            nj = 8 if (seg, i) in ((0, 0), (3, 3), (4, 0), (7, 3)) else 6

            def emit_S(hd):
                n_ = i * 8 + hd
                bb, sbuf_, pt = biasb[n_ % 4], Sb[n_ % 3], PT[n_ % 3]
                dma(bb.ap.rearrange("p (a b) -> p a b", a=8)[:, 0:nj, :], bias_d[slot, hd][:, 0:nj, :], [], [bb.t])
                ps = psum()
                p_, base = hd // 2, (hd % 2) * 64
                for j in range(nj):
                    kt = ktile(i, j)
                    mm(ps.ap[:, j * 128:(j + 1) * 128], KT.ap[base:base + 64, p_, kt * 128:(kt + 1) * 128],
                       QT.ap[base:base + 64, p_, i * 128:(i + 1) * 128], True, True, [KT.t, QT.t], [ps.t])
                stt(sbuf_.ap[:, 0:nj * 128], ps.ap[:, 0:nj * 128], 0.125, bb.ap[:, 0:nj * 128], ALU.mult, ALU.add,
                    [ps.t, bb.t], [sbuf_.t])
                act(pt.ap.rearrange("p a b -> p (a b)")[:, 0:nj * 128], sbuf_.ap[:, 0:nj * 128], AF.Exp,
                    [sbuf_.t], [pt.t])

            def emit_PV(hd):
                pt = PT[(i * 8 + hd) % 3]
                for j in range(nj):
                    kt = ktile(i, j)
                    mm(po.ap[:, hd * 128:hd * 128 + 65], pt.ap[:, j, :], V1.ap[:, kt, hd, :], j == 0, j == nj - 1,
                       [pt.t, V1.t], [po.t])

            emit_S(0)
            emit_S(1)
            for hd in range(8):
                if hd + 2 < 8:
                    emit_S(hd + 2)
                emit_PV(hd)

        def tail_a(i):
            pov = po.ap.rearrange("p (h d) -> p h d", h=8)
            bout = bouts[i]
            S.op('dve', lambda h: h.reciprocal(out=rden.ap.unsqueeze(2), in_=pov[:, :, 64:65]), [po.t], [rden.t])
            tt('dve', bout.ap, pov[:, :, 0:64], rden.ap.unsqueeze(2).to_broadcast([128, 8, 64]), ALU.mult,
               [po.t, rden.t], [bout.t])

        for i in range(4):
            att(i)
            tail_a(i)
            dma(bigW.ap[:, :, i * 512:(i + 1) * 512], Wqb[:, :, i * 512:(i + 1) * 512], [], [bigW.t])
        pbn[0] = 4
        for i in range(4):
            pb = psum()
            boutf = bouts[i].ap.rearrange("p h d -> p (h d)")
            for kk in range(4):
                tr(pb.ap[:, kk * 128:(kk + 1) * 128], boutf[:, kk * 128:(kk + 1) * 128], [bouts[i].t], [pb.t])
            tcopy('act', mixTs[i].ap, pb.ap[:, 0:512].rearrange("p (a b) -> p a b", a=4), [pb.t], [mixTs[i].t])
        pms = []
        for i in range(4):
            pm = psum()
            pms.append(pm)
            for nh in range(2):
                for k in range(8):
                    if k < 4:
                        lhsT, ltok = aT.ap[:, k, i * 128:(i + 1) * 128], aT.t
                    else:
                        lhsT, ltok = mixTs[i].ap[:, k - 4, :], mixTs[i].t
                    mm(pm.ap[:, nh * 512:(nh + 1) * 512], lhsT, woutb.ap[:, k, nh * 512:(nh + 1) * 512],
                       k == 0, k == 7, [ltok, woutb.t], [pm.t])
        h2_ts = [Tok() for _ in range(4)]
        smB = []
        for i in range(4):
            stt(pres[i].ap, h0own.ap[:, i, :], ALPHA, pms[i].ap, ALU.mult, ALU.add, [h0own.t, pms[i].t], [pres[i].t])
            smB.append(ln_stats(pres[i].ap, [pres[i].t]))
        for i in range(4):
            o_ap = h2s.ap[:, i, :]
            ts('dve', o_ap, pres[i].ap, smB[i].ap[:, 12:13], smB[i].ap[:, 14:15], ALU.subtract, ALU.mult,
               [pres[i].t, smB[i].t], [h2_ts[i]])
            tt('dve', o_ap, o_ap, lng.ap, ALU.mult, [h2_ts[i], lng.t], [h2_ts[i]])
            tt('dve', o_ap, o_ap, lnb.ap, ALU.add, [h2_ts[i], lnb.t], [h2_ts[i], h2s.t])
        for i in range(4):
            transpose8(h2s.ap[:, i, :], h2_ts[i], lambda k0, k1, i=i: h2T.ap[:, k0:k1, i * 128:(i + 1) * 128], h2T.t)
            if DEBUG_H2 and DEBUG_WHAT == 'h2':
                dma(dbg[seg * 512 + i * 128: seg * 512 + (i + 1) * 128, :], h2s.ap[:, i, :], [h2_ts[i]], [])
        pbn[0] = 4
        S.barrier()
        if STOP_AFTER == 'B':
            continue

        cv = Carver(limit=OFF_W)
        bigW = cv.at(OFF_W, [128, 8, 2048], BF16)
        b2 = []
        for _par in range(2):
            b2.append(dict(qT=cv.get([128, 16, 128], BF16), Ssb=cv.get([128, 2048], F32), S2=cv.get([128, 2048], F32),
                           Ssb_ts=[Tok() for _ in range(16)], S2_ts=[Tok() for _ in range(16)]))
        Tv = cv.get([128, 16, 16], F32)
        Ti = cv.get([128, 16, 16], U32)
        Tif = cv.get([128, 16, 16], F32)
        TSv = cv.get([128, 8, 16], F32)
        PI = cv.get([128, 8, 16], U32)
        PIf = cv.get([128, 8, 16], F32)
        Jf = cv.get([128, 8, 16], F32)
        Kf = cv.get([128, 8, 16], F32)
        i1f = cv.get([128, 128], F32)
        i2f = cv.get([128, 128], F32)
        Gt = cv.get([128, 128], F32)
        Zs = cv.get([128, 8], F32)
        Tv4 = Tv.ap.rearrange("p (h two) j -> p h two j", two=2)
        Tif4 = Tif.ap.rearrange("p (h two) j -> p h two j", two=2)
        Gt3 = Gt.ap.rearrange("p (h r) -> p h r", h=8)
        Tv_ts = [Tok() for _ in range(16)]
        Ti_ts = [Tok() for _ in range(16)]
        TSv_ts = [Tok() for _ in range(8)]
        PI_ts = [Tok() for _ in range(8)]

        def b2_front(i):
            B_ = b2[i % 2]
            qT, Ssb, S2, Ssb_ts, S2_ts = B_['qT'], B_['Ssb'], B_['S2'], B_['Ssb_ts'], B_['S2_ts']
            Ssb3 = Ssb.ap.rearrange("p (g n) -> p g n", g=16)
            S23 = S2.ap.rearrange("p (g n) -> p g n", g=16)
            cand4 = Ssb.ap.rearrange("p (h j k) -> p h j k", h=8, j=16)
            candf = Ssb.ap.rearrange("p (h c) -> p h c", h=8)
            cand2f = S2.ap.rearrange("p (h c) -> p h c", h=8)
            eq4 = S2.ap.rearrange("p (h j k) -> p h j k", h=8, j=16)
            for g in range(16):
                pb = psum()
                for k in range(8):
                    mm(pb.ap[:, 0:128], bigW.ap[:, k, g * 128:(g + 1) * 128], h2T.ap[:, k, i * 128:(i + 1) * 128],
                       k == 0, k == 7, [bigW.t, h2T.t], [pb.t])
                tcopy('act', qT.ap[:, g, :], pb.ap[:, 0:128], [pb.t], [qT.t])
            for half in range(2):
                pb = psum()
                for gg in range(8):
                    g = half * 8 + gg
                    mm(pb.ap[:, gg * 128:(gg + 1) * 128], qT.ap[:, g, :], keysb.ap[:, g, :],
                       True, True, [qT.t, keysb.t], [pb.t])
                tcopy('act', Ssb.ap[:, half * 1024:(half + 1) * 1024], pb.ap, [pb.t], Ssb_ts[half * 8:(half + 1) * 8])

        def b2_chain(i):
            B_ = b2[i % 2]
            qT, Ssb, S2, Ssb_ts, S2_ts = B_['qT'], B_['Ssb'], B_['S2'], B_['Ssb_ts'], B_['S2_ts']
            Ssb3 = Ssb.ap.rearrange("p (g n) -> p g n", g=16)
            S23 = S2.ap.rearrange("p (g n) -> p g n", g=16)
            cand4 = Ssb.ap.rearrange("p (h j k) -> p h j k", h=8, j=16)
            candf = Ssb.ap.rearrange("p (h c) -> p h c", h=8)
            cand2f = S2.ap.rearrange("p (h c) -> p h c", h=8)
            eq4 = S2.ap.rearrange("p (h j k) -> p h j k", h=8, j=16)
            for g in range(16):
                S.op('dve', lambda h, g=g: h.max(out=Tv.ap[:, g, 0:8], in_=Ssb3[:, g, :]), [Ssb_ts[g]], [Tv_ts[g]])
            for g in range(16):
                S.op('dve', lambda h, g=g: h.max_index(out=Ti.ap[:, g, 0:8], in_max=Tv.ap[:, g, 0:8],
                                                       in_values=Ssb3[:, g, :]), [Ssb_ts[g], Tv_ts[g]], [Ti_ts[g]])
            for g in range(16):
                S.op('dve', lambda h, g=g: h.match_replace(out=S23[:, g, :], in_to_replace=Tv.ap[:, g, 0:8],
                                                           in_values=Ssb3[:, g, :], imm_value=NEG),
                     [Ssb_ts[g], Tv_ts[g]], [S2_ts[g]])
            for g in range(16):
                S.op('dve', lambda h, g=g: h.max(out=Tv.ap[:, g, 8:16], in_=S23[:, g, :]), [S2_ts[g]], [Tv_ts[g]])
            for g in range(16):
                S.op('dve', lambda h, g=g: h.max_index(out=Ti.ap[:, g, 8:16], in_max=Tv.ap[:, g, 8:16],
                                                       in_values=S23[:, g, :]), [S2_ts[g], Tv_ts[g]], [Ti_ts[g]])
            tcopy('dve', Tif.ap, Ti.ap, Ti_ts, [Tif.t])
            tt('dve', cand4, Tv4[:, :, 0, :].unsqueeze(3).to_broadcast([128, 8, 16, 16]),
               Tv4[:, :, 1, :].unsqueeze(2).to_broadcast([128, 8, 16, 16]), ALU.add, Tv_ts, Ssb_ts)
            hs = lambda L, hh: [L[2 * hh], L[2 * hh + 1]]
            for hh in range(8):
                S.op('dve', lambda h, hh=hh: h.max(out=TSv.ap[:, hh, 0:8], in_=candf[:, hh, :]),
                     hs(Ssb_ts, hh), [TSv_ts[hh]])
            for hh in range(8):
                S.op('dve', lambda h, hh=hh: h.max_index(out=PI.ap[:, hh, 0:8], in_max=TSv.ap[:, hh, 0:8],
                                                         in_values=candf[:, hh, :]),
                     hs(Ssb_ts, hh) + [TSv_ts[hh]], [PI_ts[hh]])
            for hh in range(8):
                S.op('dve', lambda h, hh=hh: h.match_replace(out=cand2f[:, hh, :], in_to_replace=TSv.ap[:, hh, 0:8],
                                                             in_values=candf[:, hh, :], imm_value=NEG),
                     hs(Ssb_ts, hh) + [TSv_ts[hh]], hs(S2_ts, hh))
            for hh in range(8):
                S.op('dve', lambda h, hh=hh: h.max(out=TSv.ap[:, hh, 8:16], in_=cand2f[:, hh, :]),
                     hs(S2_ts, hh), [TSv_ts[hh]])
            for hh in range(8):
                S.op('dve', lambda h, hh=hh: h.max_index(out=PI.ap[:, hh, 8:16], in_max=TSv.ap[:, hh, 8:16],
                                                         in_values=cand2f[:, hh, :]),
                     hs(S2_ts, hh) + [TSv_ts[hh]], [PI_ts[hh]])
            tt('dve', Gt3, TSv.ap, TSv.ap[:, :, 0:1].to_broadcast([128, 8, 16]), ALU.subtract, TSv_ts, [Gt.t])
            act(Gt.ap, Gt.ap, AF.Exp, [Gt.t], [Gt.t])
            S.op('dve', lambda h: h.tensor_reduce(out=Zs.ap, in_=Gt3, axis=AX.X, op=ALU.add), [Gt.t], [Zs.t])
            S.op('dve', lambda h: h.reciprocal(out=Zs.ap, in_=Zs.ap), [Zs.t], [Zs.t])
            tt('dve', Gt3, Gt3, Zs.ap.unsqueeze(2).to_broadcast([128, 8, 16]), ALU.mult, [Gt.t, Zs.t], [Gt.t])
            tcopy('dve', PIf.ap, PI.ap, PI_ts, [PIf.t])
            thr4 = iota16.ap[:, 16:32].unsqueeze(1).unsqueeze(1).to_broadcast([128, 8, 16, 16])
            io4 = iota16.ap[:, 0:16].unsqueeze(1).unsqueeze(1).to_broadcast([128, 8, 16, 16])
            ts('dve', Jf.ap, PIf.ap, 0.0625, -0.46875, ALU.mult, ALU.add, [PIf.t], [Jf.t])
            ts('dve', Jf.ap, Jf.ap, 12582912.0, None, ALU.add, None, [Jf.t], [Jf.t])
            ts('dve', Jf.ap, Jf.ap, -12582912.0, None, ALU.add, None, [Jf.t], [Jf.t])
            stt(Kf.ap, Jf.ap, -16.0, PIf.ap, ALU.mult, ALU.add, [Jf.t, PIf.t], [Kf.t])
            for (Xf, half, dst) in ((Jf, 0, i1f), (Kf, 1, i2f)):
                tt('dve', eq4, Xf.ap.unsqueeze(3).to_broadcast([128, 8, 16, 16]), io4, ALU.is_equal,
                   [Xf.t, iota16.t], S2_ts)
                tt('dve', eq4, eq4, Tif4[:, :, half, :].unsqueeze(2).to_broadcast([128, 8, 16, 16]), ALU.mult,
                   S2_ts + [Tif.t], S2_ts)
                S.op('dve', lambda h, dst=dst: h.tensor_reduce(out=dst.ap.rearrange("p (h r) -> p h r", h=8),
                                                               in_=eq4, axis=AX.X, op=ALU.add), S2_ts, [dst.t])
            pb = psum()
            for n_, srcb in enumerate((i1f, i2f, Gt)):
                tr(pb.ap[:, n_ * 128:(n_ + 1) * 128], srcb.ap, [srcb.t], [pb.t])
            for n_, dstb in enumerate((i1T, i2T, gT)):
                tcopy('act', dstb.ap[:, i * 128:(i + 1) * 128], pb.ap[:, n_ * 128:(n_ + 1) * 128], [pb.t], [dstb.t])

        b2_front(0)
        for i in range(4):
            if i + 1 < 4:
                b2_front(i + 1)
            b2_chain(i)
        S.barrier()

        cv = Carver()
        lng, lnb = load_ln(cv, 4, 5)
        Gbuf = cv.get([128, 256, 128], BF16)
        ubuf = [cv.get([128, 2, 8, 128], BF16) for _ in range(4)]
        vbuf = [cv.get([128, 2, 1024], BF16) for _ in range(4)]
        OH1 = [cv.get([128, 128, 16], BF16) for _ in range(2)]
        OH2 = [cv.get([128, 128, 16], BF16) for _ in range(2)]
        Ag = [cv.get([128, 256], F32) for _ in range(2)]
        AG = [cv.get([128, 256], BF16) for _ in range(2)]
        pre2 = [cv.get([128, 1024], F32) for _ in range(2)]
        yo = [cv.get([128, 1024], F32) for _ in range(2)]
        Ubv = Ub.rearrange("p (j c i) -> p j c i", j=128, c=8)
        Vbv = Vb.rearrange("p (j d) -> p j d", j=128)
        PF = [PB[0], PB[1]]
        PR = [PB[2], PB[3]]
        pri = [0]

        def build(tb):
            t0 = tb * 256
            for c16 in range(16):
                o1, o2 = OH1[c16 % 2], OH2[c16 % 2]
                c0 = t0 + c16 * 16
                bc = lambda T_: T_.ap[:, c0:c0 + 16].unsqueeze(1).to_broadcast([128, 128, 16])
                tt('dve', o1.ap, iotaR.ap, bc(i1T), ALU.is_equal, [iotaR.t, i1T.t], [o1.t])
                tt('dve', o1.ap, o1.ap, bc(gT), ALU.mult, [o1.t, gT.t], [o1.t])
                tt('dve', o2.ap, iotaR.ap, bc(i2T), ALU.is_equal, [iotaR.t, i2T.t], [o2.t])
                for q8 in range(2):
                    pr = PR[pri[0] % 2]
                    pri[0] += 1
                    for tk in range(8):
                        tl = q8 * 8 + tk
                        mm(pr.ap[:, tk * 128:(tk + 1) * 128], o1.ap[:, :, tl], o2.ap[:, :, tl], True, True,
                           [o1.t, o2.t], [pr.t])
                    tl0 = c16 * 16 + q8 * 8
                    tcopy('act', Gbuf.ap[:, tl0:tl0 + 8, :], pr.ap.rearrange("p (a b) -> p a b", a=8),
                          [pr.t], [Gbuf.t])

        def load_grp(gg):
            ub, vb = ubuf[gg % 4], vbuf[gg % 4]
            dma(ub.ap, Ubv[:, gg * 2:(gg + 1) * 2, :, :], [], [ub.t], eng='sp')
            dma(vb.ap, Vbv[:, gg * 2:(gg + 1) * 2, :], [], [vb.t], eng='sp')

        def sweep(tb, hook):
            t0 = tb * 256

            def emit_AT(j):
                gg, jj = j // 2, j % 2
                ub = ubuf[gg % 4]
                pr = PR[j % 2]
                ag, agb = Ag[j % 2], AG[j % 2]
                for k in range(8):
                    mm(pr.ap[:, 0:256], ub.ap[:, jj, k, :], h2T.ap[:, k, t0:t0 + 256], k == 0, k == 7,
                       [ub.t, h2T.t], [pr.t])
                act(ag.ap, pr.ap[:, 0:256], AF.Gelu, [pr.t], [ag.t])
                tt('dve', agb.ap, ag.ap, Gbuf.ap[:, :, j], ALU.mult, [ag.t, Gbuf.t], [agb.t])

            def emit_F(j):
                gg, jj = j // 2, j % 2
                vb = vbuf[gg % 4]
                agb = AG[j % 2]
                for tt_ in range(2):
                    for nh in range(2):
                        mm(PF[tt_].ap[:, nh * 512:(nh + 1) * 512], agb.ap[:, tt_ * 128:(tt_ + 1) * 128],
                           vb.ap[:, jj, nh * 512:(nh + 1) * 512], j == 0, j == 127,
                           [agb.t, vb.t], [PF[tt_].t])

            for gg in range(4):
                load_grp(gg)
            emit_AT(0)
            for j in range(128):
                if j + 1 < 128:
                    emit_AT(j + 1)
                emit_F(j)
                if j % 2 == 1 and j // 2 + 4 < 64:
                    load_grp(j // 2 + 4)
                if j == 12 and hook is not None:
                    hook()

        def fin_pre(tb):
            for tt_ in range(2):
                i = tb * 2 + tt_
                stt(pre2[tt_].ap, h2s.ap[:, i, :], ALPHA, PF[tt_].ap, ALU.mult, ALU.add,
                    [h2s.t, PF[tt_].t], [pre2[tt_].t])

        def fin_ln(tb):
            for tt_ in range(2):
                i = tb * 2 + tt_
                yb = yo[tt_]
                sm = ln_stats(pre2[tt_].ap, [pre2[tt_].t])
                ts('dve', yb.ap, pre2[tt_].ap, sm.ap[:, 12:13], sm.ap[:, 14:15], ALU.subtract, ALU.mult,
                   [pre2[tt_].t, sm.t], [yb.t])
                tt('dve', yb.ap, yb.ap, lng.ap, ALU.mult, [yb.t, lng.t], [yb.t])
                tt('dve', yb.ap, yb.ap, lnb.ap, ALU.add, [yb.t, lnb.t], [yb.t])
                r0 = seg * 512 + i * 128
                dma(y[r0:r0 + 128, :], yb.ap, [yb.t], [])

        build(0)
        sweep(0, None)
        build(1)
        fin_pre(0)
        sweep(1, lambda: fin_ln(0))
        fin_pre(1)
        fin_ln(1)
        S.barrier()

    S.emit(nc, es)
    es.close()
    return nc


def _bias_table(rpb, rq0, R):
    tab = np.full((8, 128, 8, 128), NEG, np.float32)
    q = np.arange(128)
    rq = rq0 + q // 64
    cq = q % 64
    rs = np.clip(rq - 4, 0, R - 8)
    cs = np.clip(cq - 8, 0, 48)
    k = np.arange(128)
    ck = k % 64
    order = {-2: 0, -1: 1, 0: 2, 1: 3, 2: 4, -3: 6, 3: 7}
    for dlt, j in order.items():
        rk = rq0 + 2 * dlt + k // 64
        row_ok = (rk[:, None] >= rs[None, :]) & (rk[:, None] < rs[None, :] + 8) & (rk[:, None] >= 0) & (rk[:, None] < R)
        col_ok = (ck[:, None] >= cs[None, :]) & (ck[:, None] < cs[None, :] + 16)
        ok = row_ok & col_ok
        ro = np.clip(rk[:, None] - rq[None, :] + 7, 0, 14)
        co = np.clip(ck[:, None] - cq[None, :] + 15, 0, 30)
        vals = rpb[:, ro, co]
        tab[:, :, j, :] = np.where(ok[None], vals, np.float32(NEG))
    tab[:, 0:16, 5, :] = 0.0
    return tab


def _seg_tokens(xseq, meta, start):
    T, D = xseq.shape
    out = np.zeros((NT, 128, D), np.float32)
    out[0, 0:16] = meta
    if start == 0:
        out[0, 16:24] = meta[8:16]
    else:
        out[0, 16:24] = xseq[start - 8:start]
    e = start + 512
    n_post = min(8, T - e)
    if n_post > 0:
        out[0, 24:24 + n_post] = xseq[e:e + n_post]
    flat = out[1:].reshape(8 * 128, D)
    lo = start - 256
    hi = start + 512 + 256
    a, b = max(lo, 0), min(hi, T)
    flat[a - lo:b - lo] = xseq[a:b]
    return out, n_post


def _prepare(inputs):
    f = lambda k: np.asarray(inputs[k], np.float32)
    xp, xs, meta = f("x_prompt"), f("x_sample"), f("meta_tokens")
    rpb = f("nat_rpb")[0]
    shared = {}
    shared["w_in"] = np.ascontiguousarray(f("w_in")[0])
    shared["w_out"] = np.ascontiguousarray(f("w_out")[0])
    shared["wq"] = np.ascontiguousarray(f("peer_wq")[0])
    k1, k2 = f("peer_key1")[0], f("peer_key2")[0]
    keys = np.stack([k1, k2], axis=1).reshape(16, 128, 128)
    shared["keysT"] = np.ascontiguousarray(keys.transpose(2, 0, 1))
    shared["pool_w"] = np.ascontiguousarray(f("pool_w")[0])
    shared["pscale"] = np.ascontiguousarray(f("pool_scale")[0].reshape(4, 128).T)
    lt = np.stack([f("emb_ln_g"), f("emb_ln_b"), f("ln1_g")[0], f("ln1_b")[0], f("ln2_g")[0], f("ln2_b")[0]])
    shared["lntab"] = np.ascontiguousarray(np.broadcast_to(lt[:, None, :], (6, 128, 1024)))
    shared["ident"] = np.eye(128, dtype=np.float32)
    shared["iota128"] = np.broadcast_to(np.arange(128, dtype=np.float32)[None], (128, 128)).astype(ml_dtypes.bfloat16)
    io = np.concatenate([np.arange(16, dtype=np.float32), 16.0 * (np.arange(16, dtype=np.float32) + 1.0)])
    shared["iota16"] = np.ascontiguousarray(np.broadcast_to(io[None], (128, 32)))
    rep = np.repeat(np.arange(128, dtype=np.float32), 16)
    shared["iotarep"] = np.ascontiguousarray(np.broadcast_to(rep[None], (128, 2048))).astype(ml_dtypes.bfloat16)
    u, v = f("peer_u")[0], f("peer_v")[0]
    shared["UT"] = np.ascontiguousarray(u.reshape(128, 128, 8, 128).transpose(3, 1, 2, 0)).reshape(128, 131072)
    shared["Vt"] = np.ascontiguousarray(v.reshape(128, 131072))
    wins = [2, 4, 8, 16]
    in_maps = []
    for c in range(8):
        m = dict(shared)
        s_, qtr = c // 4, c % 4
        xin = np.zeros((NSEG, NT, 128, 1024), np.float32)
        postv = np.zeros((128, NSEG, 8), np.float32)
        invc = np.zeros((128, NSEG, 4, 8), np.float32)
        tabs = np.zeros((NSLOT, 8, 128, 8, 128), np.float32)
        tabs[0] = _bias_table(rpb, 8, 32)
        for seg in range(NSEG):
            if seg < 4:
                xseq, start, R = xp[c], seg * 512, 32
            else:
                xseq, start, R = xs[s_], qtr * 2048 + (seg - 4) * 512, 128
            T = xseq.shape[0]
            xin[seg], n_post = _seg_tokens(xseq, meta, start)
            postv[:, seg, :n_post] = 1.0
            L = 16 + T
            for g, w_ in enumerate(wins):
                pos = 16 + start + 504 + np.arange(8)
                lo = np.clip(pos - w_ // 2, 0, L - 1)
                hi = np.clip(pos - w_ // 2 + w_ - 1, 0, L - 1)
                invc[:, seg, g, :] = (1.0 / (hi - lo + 1).astype(np.float32))[None, :]
            row0 = start // 64
            for i in range(4):
                slot = 0
                if seg == 0 and i < 2:
                    slot = 1 + i
                elif seg == 3 and i >= 2:
                    slot = 1 + i
                elif seg == 4 and i < 2:
                    slot = 5 + i
                elif seg == 7 and i >= 2:
                    slot = 5 + i
                if slot:
                    tabs[slot] = _bias_table(rpb, row0 + 2 * i, R)
        m["xin"] = xin
        m["postv"] = postv
        m["invc"] = invc
        m["biastab"] = tabs
        in_maps.append(m)
    return in_maps


_NC_CACHE = {}


def kernel(**inputs):
    in_maps = _prepare(inputs)
    if "nc" not in _NC_CACHE:
        _NC_CACHE["nc"] = build_program()
    nc = _NC_CACHE["nc"]
    res = run_bass_kernel_spmd(nc, in_maps, core_ids=list(range(8)))
    ys = [np.asarray(r["y"], np.float32) for r in res.results]
    y_prompt = np.stack([ys[c][0:2048] for c in range(8)], axis=0)
    y_sample = np.stack([np.concatenate([ys[s * 4 + q][2048:4096] for q in range(4)], axis=0) for s in range(2)], axis=0)
    if DEBUG_H2:
        kernel.dbg = [np.asarray(r["dbg"], np.float32) for r in res.results]
    return (y_prompt, y_sample)
```
